# Optimizing a Trainium2 kernel written in Bass

```python
import math
import jax, jax.numpy as jnp
from jax import lax
import numpy as np

D_MODEL = 2048
BATCH = 8
SEQ = 4096
DEPTH = 4

N_A_LAYERS = DEPTH // 2
N_B_LAYERS = DEPTH - N_A_LAYERS
GDN_HEADS = 16
GDN_DK = 128
GDN_DV = 128
GDN_QK_WIDTH = GDN_HEADS * GDN_DK
GDN_V_WIDTH = GDN_HEADS * GDN_DV
GDN_CONV_CH = 2 * GDN_QK_WIDTH + GDN_V_WIDTH
GDN_PROJ = GDN_CONV_CH + GDN_V_WIDTH + 2 * GDN_HEADS
CONV_WIDTH = 4
CHUNK = 64
GDN_EPS = 1e-6
DIFF_HEADS = 16
DIFF_DK = D_MODEL // DIFF_HEADS // 2
DIFF_DV = 2 * DIFF_DK
DIFF_Q_WIDTH = 2 * DIFF_HEADS * DIFF_DK
DIFF_V_WIDTH = DIFF_HEADS * DIFF_DV
Q_BLOCK = 128
SUBLN_EPS = 1e-5
D_FF = 4 * D_MODEL
ALPHA = (2 * DEPTH) ** 0.25
BETA = (8 * DEPTH) ** -0.25
LN_EPS = 1e-5

kernel_name = "yoco_gdn_diffattn_hybrid"


def layer_norm(x, g, b):
    xf = x.astype(jnp.float32)
    mu = jnp.mean(xf, axis=-1, keepdims=True)
    var = jnp.mean(jnp.square(xf - mu), axis=-1, keepdims=True)
    return ((xf - mu) * lax.rsqrt(var + LN_EPS) * g + b).astype(x.dtype)


def rms_norm(x, w, eps):
    xf = x.astype(jnp.float32)
    return (xf * lax.rsqrt(jnp.mean(jnp.square(xf), axis=-1, keepdims=True) + eps) * w).astype(x.dtype)


def l2_normalize(x):
    xf = x.astype(jnp.float32)
    return xf * lax.rsqrt(jnp.sum(jnp.square(xf), axis=-1, keepdims=True) + GDN_EPS)


def causal_conv_silu(u, w):
    S = u.shape[1]
    K = w.shape[0]
    up = jnp.pad(u, ((0, 0), (K - 1, 0), (0, 0)))
    y = up[:, 0:S] * w[0]
    for j in range(1, K):
        y = y + up[:, j:j + S] * w[j]
    return jax.nn.silu(y)


def gated_delta_rule_chunked(q, k, v, g, beta):
    B, H, S, dk = q.shape
    dv = v.shape[-1]
    N = S // CHUNK
    q = q.reshape(B, H, N, CHUNK, dk)
    k = k.reshape(B, H, N, CHUNK, dk)
    v = v.reshape(B, H, N, CHUNK, dv)
    beta = beta.reshape(B, H, N, CHUNK)
    g = jnp.cumsum(g.reshape(B, H, N, CHUNK), axis=-1)
    causal = jnp.tril(jnp.ones((CHUNK, CHUNK), dtype=bool))
    strict = jnp.tril(jnp.ones((CHUNK, CHUNK), dtype=bool), k=-1)
    decay = jnp.exp(jnp.where(causal, g[..., :, None] - g[..., None, :], -jnp.inf))
    k_beta = k * beta[..., None]
    lower = jnp.where(strict, jnp.einsum('bhnid,bhnjd->bhnij', k_beta, k) * decay, 0.0)
    tri = lower + jnp.eye(CHUNK, dtype=lower.dtype)
    rhs = jnp.concatenate([v * beta[..., None], k_beta * jnp.exp(g)[..., None]], axis=-1)
    sol = lax.linalg.triangular_solve(tri, rhs, left_side=True, lower=True, unit_diagonal=True)
    u, w = sol[..., :dv], sol[..., dv:]
    attn_intra = jnp.einsum('bhnid,bhnjd->bhnij', q, k) * decay

    def step(state, inp):
        q_n, k_n, u_n, w_n, g_n, a_n = inp
        v_new = u_n - jnp.einsum('bhck,bhkv->bhcv', w_n, state)
        o = (jnp.einsum('bhck,bhkv->bhcv', q_n * jnp.exp(g_n)[..., None], state)
             + jnp.einsum('bhij,bhjv->bhiv', a_n, v_new))
        g_last = g_n[..., -1]
        state = (state * jnp.exp(g_last)[..., None, None]
                 + jnp.einsum('bhck,bhcv->bhkv', k_n * jnp.exp(g_last[..., None] - g_n)[..., None], v_new))
        return state, o

    to_front = lambda t: jnp.moveaxis(t, 2, 0)
    xs = (to_front(q), to_front(k), to_front(u), to_front(w), to_front(g), to_front(attn_intra))
    state0 = jnp.zeros((B, H, dk, dv), jnp.float32)
    _, o = lax.scan(step, state0, xs)
    return jnp.moveaxis(o, 0, 2).reshape(B, H, S, dv)


def gated_deltanet(x, w_in, conv_w, a_log, dt_bias, norm_w, w_out):
    B, S, _ = x.shape
    p = x @ w_in
    qkv = causal_conv_silu(p[..., :GDN_CONV_CH], conv_w)
    z = p[..., GDN_CONV_CH:GDN_CONV_CH + GDN_V_WIDTH]
    b = p[..., GDN_CONV_CH + GDN_V_WIDTH:GDN_CONV_CH + GDN_V_WIDTH + GDN_HEADS]
    a = p[..., GDN_CONV_CH + GDN_V_WIDTH + GDN_HEADS:]
    q = l2_normalize(qkv[..., :GDN_QK_WIDTH].reshape(B, S, GDN_HEADS, GDN_DK)) * (GDN_DK ** -0.5)
    k = l2_normalize(qkv[..., GDN_QK_WIDTH:2 * GDN_QK_WIDTH].reshape(B, S, GDN_HEADS, GDN_DK))
    v = qkv[..., 2 * GDN_QK_WIDTH:].reshape(B, S, GDN_HEADS, GDN_DV).astype(jnp.float32)
    beta = jax.nn.sigmoid(b.astype(jnp.float32))
    g = -jnp.exp(a_log.astype(jnp.float32)) * jax.nn.softplus(a.astype(jnp.float32) + dt_bias.astype(jnp.float32))
    tr = lambda t: jnp.swapaxes(t, 1, 2)
    o = gated_delta_rule_chunked(tr(q), tr(k), tr(v), tr(g), tr(beta))
    o = jnp.swapaxes(o, 1, 2).astype(x.dtype)
    o = rms_norm(o, norm_w, GDN_EPS) * jax.nn.silu(z.reshape(B, S, GDN_HEADS, GDN_DV))
    return o.reshape(B, S, GDN_V_WIDTH) @ w_out


def shared_kv(x, w_kv):
    B, S, _ = x.shape
    kv = x @ w_kv
    k = kv[..., :DIFF_Q_WIDTH].reshape(B, S, DIFF_HEADS, 2, DIFF_DK).transpose(0, 2, 3, 1, 4)
    v = kv[..., DIFF_Q_WIDTH:].reshape(B, S, DIFF_HEADS, DIFF_DV).transpose(0, 2, 1, 3)
    return k, v


def diff_attention(x, w_q, lam_params, subln_w, w_o, k_sh, v_sh, lambda_init):
    B, S, _ = x.shape
    nb = S // Q_BLOCK
    q = (x @ w_q).reshape(B, S, DIFF_HEADS, 2, DIFF_DK).transpose(0, 2, 3, 1, 4)
    qb = q.reshape(B, DIFF_HEADS, 2, nb, Q_BLOCK, DIFF_DK).transpose(3, 0, 1, 2, 4, 5)
    lp = lam_params.astype(jnp.float32)
    lam = jnp.exp(jnp.sum(lp[0] * lp[1])) - jnp.exp(jnp.sum(lp[2] * lp[3])) + lambda_init
    k_pos = jnp.arange(S)
    scale = DIFF_DK ** -0.5

    def block(args):
        q_i, i = args
        s = jnp.einsum('bhcqd,bhckd->bhcqk', q_i, k_sh).astype(jnp.float32) * scale
        q_pos = i * Q_BLOCK + jnp.arange(Q_BLOCK)
        s = jnp.where(k_pos[None, :] <= q_pos[:, None], s, -jnp.inf)
        p = jax.nn.softmax(s, axis=-1)
        attn = p[:, :, 0] - lam * p[:, :, 1]
        return jnp.einsum('bhqk,bhkv->bhqv', attn.astype(v_sh.dtype), v_sh)

    o = lax.map(block, (qb, jnp.arange(nb)))
    o = o.transpose(1, 2, 0, 3, 4).reshape(B, DIFF_HEADS, S, DIFF_DV)
    o = rms_norm(o, subln_w, SUBLN_EPS) * (1.0 - lambda_init)
    return o.transpose(0, 2, 1, 3).reshape(B, S, DIFF_V_WIDTH) @ w_o


def sq_relu_mlp(x, w_up, w_down):
    return jnp.square(jax.nn.relu(x @ w_up)) @ w_down


def setup_inputs(seed: int = 0) -> dict:
    key = jax.random.key(seed)
    ks = jax.random.split(key, 16)
    nrm = lambda k, shape, scale: jax.random.normal(k, shape, jnp.float32) * scale
    x = nrm(ks[0], (BATCH, SEQ, D_MODEL), 1.0)
    gdn_w_in = nrm(ks[1], (N_A_LAYERS, D_MODEL, GDN_PROJ), D_MODEL ** -0.5)
    gdn_conv_w = nrm(ks[2], (N_A_LAYERS, CONV_WIDTH, GDN_CONV_CH), CONV_WIDTH ** -0.5)
    gdn_a_log = jnp.log(jax.random.uniform(ks[3], (N_A_LAYERS, GDN_HEADS), jnp.float32, 1.0, 16.0))
    dt = jnp.exp(jax.random.uniform(ks[4], (N_A_LAYERS, GDN_HEADS), jnp.float32, math.log(1e-3), math.log(1e-1)))
    gdn_dt_bias = dt + jnp.log(-jnp.expm1(-dt))
    gdn_norm_w = 1.0 + nrm(ks[5], (N_A_LAYERS, GDN_DV), 0.02)
    gdn_w_out = nrm(ks[6], (N_A_LAYERS, GDN_V_WIDTH, D_MODEL), GDN_V_WIDTH ** -0.5 * BETA)
    diff_w_q = nrm(ks[7], (N_B_LAYERS, D_MODEL, DIFF_Q_WIDTH), D_MODEL ** -0.5)
    diff_lambda = nrm(ks[8], (N_B_LAYERS, 4, DIFF_DK), 0.1)
    diff_subln_w = 1.0 + nrm(ks[9], (N_B_LAYERS, DIFF_DV), 0.02)
    diff_w_o = nrm(ks[10], (N_B_LAYERS, DIFF_V_WIDTH, D_MODEL), DIFF_V_WIDTH ** -0.5 * BETA)
    shared_w_kv = nrm(ks[11], (D_MODEL, DIFF_Q_WIDTH + DIFF_V_WIDTH), D_MODEL ** -0.5)
    mlp_w_up = nrm(ks[12], (DEPTH, D_MODEL, D_FF), D_MODEL ** -0.5)
    mlp_w_down = nrm(ks[13], (DEPTH, D_FF, D_MODEL), D_FF ** -0.5 * BETA)
    ln_g = 1.0 + nrm(ks[14], (DEPTH, 2, D_MODEL), 0.02)
    ln_b = nrm(ks[15], (DEPTH, 2, D_MODEL), 0.02)
    return {"x": x, "gdn_w_in": gdn_w_in, "gdn_conv_w": gdn_conv_w, "gdn_a_log": gdn_a_log,
            "gdn_dt_bias": gdn_dt_bias, "gdn_norm_w": gdn_norm_w, "gdn_w_out": gdn_w_out,
            "diff_w_q": diff_w_q, "diff_lambda": diff_lambda, "diff_subln_w": diff_subln_w,
            "diff_w_o": diff_w_o, "shared_w_kv": shared_w_kv, "mlp_w_up": mlp_w_up,
            "mlp_w_down": mlp_w_down, "ln_g": ln_g, "ln_b": ln_b}


def reference(x, gdn_w_in, gdn_conv_w, gdn_a_log, gdn_dt_bias, gdn_norm_w, gdn_w_out,
              diff_w_q, diff_lambda, diff_subln_w, diff_w_o, shared_w_kv, mlp_w_up,
              mlp_w_down, ln_g, ln_b):
    k_sh = None
    v_sh = None
    for l in range(DEPTH):
        if l < N_A_LAYERS:
            h = gated_deltanet(x, gdn_w_in[l], gdn_conv_w[l], gdn_a_log[l], gdn_dt_bias[l],
                               gdn_norm_w[l], gdn_w_out[l])
        else:
            j = l - N_A_LAYERS
            lambda_init = 0.8 - 0.6 * math.exp(-0.3 * l)
            h = diff_attention(x, diff_w_q[j], diff_lambda[j], diff_subln_w[j], diff_w_o[j],
                               k_sh, v_sh, lambda_init)
        x = layer_norm(ALPHA * x + h, ln_g[l, 0], ln_b[l, 0])
        x = layer_norm(ALPHA * x + sq_relu_mlp(x, mlp_w_up[l], mlp_w_down[l]), ln_g[l, 1], ln_b[l, 1])
        if l == N_A_LAYERS - 1:
            k_sh, v_sh = shared_kv(x, shared_w_kv)
    return x
```

```python
import numpy as np
import concourse.bass as bass
import concourse.mybir as mybir
from concourse.bass_utils import run_bass_kernel_spmd

F32 = mybir.dt.float32
BF16 = mybir.dt.bfloat16
U8 = mybir.dt.uint8
AF = mybir.ActivationFunctionType
ALU = mybir.AluOpType
AX = mybir.AxisListType


class Buf:
    __slots__ = ("name", "w", "r")

    def __init__(self, name=""):
        self.name = name
        self.w = None
        self.r = {}


class V:
    __slots__ = ("ap", "toks")

    def __init__(self, ap, toks):
        self.ap = ap
        self.toks = toks

    def __getitem__(self, idx):
        return V(self.ap[idx], self.toks)

    def re(self, pat, **kw):
        return V(self.ap.rearrange(pat, **kw), self.toks)

    def bc(self, dt):
        return V(self.ap.bitcast(dt), self.toks)


class Chan:
    __slots__ = ("sem", "count")

    def __init__(self, sem):
        self.sem = sem
        self.count = 0


class Op:
    __slots__ = ("eng", "fn", "deps", "sem", "val", "sig", "chan", "idx", "dmaw")


ENGS = ("pe", "act", "dve", "pool", "sp")


class Prog:
    def __init__(self, nc):
        self.nc = nc
        self.q = {e: [] for e in ENGS}
        self.esem = {}
        self.nops = 0
        self.chans = []
        self.arena = None
        self.atoks = None
        self.aoff = 0
        self.banks = []
        self.bank_i = 0

    def init_mem(self, arena_bytes, gran=512):
        nc = self.nc
        self.gran = gran
        self.arena_bytes = arena_bytes
        self.arena = nc.alloc_sbuf_tensor("arena", [128, arena_bytes], U8)
        self.atoks = [Buf("a%d" % i) for i in range((arena_bytes + gran - 1) // gran)]
        for i in range(8):
            t = nc.alloc_psum_tensor("bank%d" % i, [128, 512], F32)
            self.banks.append(V(t[:, :], [Buf("bank%d" % i)]))

    def reset(self, off=0):
        self.aoff = off
        self.chan_i = 0

    def alloc(self, free_shape, dt, name=""):
        esz = 2 if dt == BF16 else 4
        n = 1
        for s in free_shape:
            n *= s
        nb = n * esz
        g = self.gran
        off = (self.aoff + g - 1) // g * g
        assert off + nb <= self.arena_bytes, ("SBUF arena overflow", name, off, nb)
        self.aoff = off + nb
        ap = self.arena[:, off:off + nb].bitcast(dt)
        if len(free_shape) == 2:
            ap = ap.rearrange("p (a b) -> p a b", a=free_shape[0])
        elif len(free_shape) == 3:
            ap = ap.rearrange("p (a b c) -> p a b c", a=free_shape[0], b=free_shape[1])
        toks = self.atoks[off // g:(off + nb + g - 1) // g]
        return V(ap, toks)

    def bank(self):
        b = self.banks[self.bank_i % 8]
        self.bank_i += 1
        return b

    def chan(self):
        i = getattr(self, "chan_i", 0)
        if i < len(self.chans):
            c = self.chans[i]
            if c.count > 0:
                op = Op()
                op.eng = "sp"
                op.fn = None
                op.sig = False
                op.chan = None
                op.sem = None
                op.val = 0
                op.idx = self.nops
                self.nops += 1
                op.deps = []
                op.dmaw = [(c.sem, c.count * 16)]
                self.q["sp"].append(op)
        else:
            c = Chan(self.nc.alloc_semaphore("ch%d" % len(self.chans)))
            self.chans.append(c)
        self.chan_i = i + 1
        return c

    def add(self, eng, fn, r=(), w=(), chan=None):
        op = Op()
        op.eng = eng
        op.fn = fn
        op.sig = False
        op.chan = chan
        op.sem = None
        op.val = 0
        op.idx = self.nops
        self.nops += 1
        deps = {}
        wt = []
        for v in w:
            wt.extend(v.toks if isinstance(v, V) else [v])
        rt = []
        for v in r:
            rt.extend(v.toks if isinstance(v, V) else [v])
        for b in rt:
            if b.w is not None:
                deps[b.w.idx] = b.w
        for b in wt:
            if b.w is not None:
                deps[b.w.idx] = b.w
            for o in b.r.values():
                deps[o.idx] = o
        for b in wt:
            b.w = op
            b.r = {}
        wset = set(id(b) for b in wt)
        dma = chan is not None
        for b in rt:
            if id(b) not in wset:
                b.r[(eng, op.idx) if dma else eng] = op
        deps.pop(op.idx, None)
        if eng == "pe":
            op.deps = [d for d in deps.values() if d.eng != "pe"]
        else:
            op.deps = list(deps.values())
        op.dmaw = [(d.chan.sem, d.chan.count * 16) for d in op.deps if d.chan is not None]
        if chan is not None:
            chan.count += 1
            op.sem = chan.sem
            op.val = chan.count * 16
        self.q[eng].append(op)
        return op

    def pe(self, fn, r=(), w=()):
        return self.add("pe", fn, r, w)

    def act(self, fn, r=(), w=()):
        return self.add("act", fn, r, w)

    def dve(self, fn, r=(), w=()):
        return self.add("dve", fn, r, w)

    def pool(self, fn, r=(), w=()):
        return self.add("pool", fn, r, w)

    def dma(self, out, in_, chan, r=(), w=(), **kw):
        oa = out.ap if isinstance(out, V) else out
        ia = in_.ap if isinstance(in_, V) else in_
        rr = list(r) + ([in_] if isinstance(in_, V) else [])
        ww = list(w) + ([out] if isinstance(out, V) else [])
        return self.add("sp", lambda e: e.dma_start(out=oa, in_=ia, **kw), rr, ww, chan)

    def emit(self, final_waits=()):
        nc = self.nc
        for e in ENGS[:4]:
            self.esem[e] = nc.alloc_semaphore("sem_" + e)
        for e in ENGS:
            for op in self.q[e]:
                for d in op.deps:
                    d.sig = True
        for e in ENGS[:4]:
            c = 0
            for op in self.q[e]:
                if op.sig:
                    c += 1
                    op.sem = self.esem[e]
                    op.val = c
        engobj = {"pe": "tensor", "act": "scalar", "dve": "vector", "pool": "gpsimd", "sp": "sync"}
        nwaits = [0]

        def run(ename, eng):
            seen = {}
            for op in self.q[ename]:
                need = {}
                for d in op.deps:
                    if d.chan is not None:
                        continue
                    k = id(d.sem)
                    if d.val > need.get(k, (None, 0))[1]:
                        need[k] = (d.sem, d.val)
                for (s_, v_) in op.dmaw:
                    k = id(s_)
                    if v_ > need.get(k, (None, 0))[1]:
                        need[k] = (s_, v_)
                for k, (s, v) in need.items():
                    if seen.get(k, 0) < v:
                        eng.wait_ge(s, v)
                        seen[k] = v
                        nwaits[0] += 1
                if op.fn is None:
                    continue
                ins = op.fn(eng)
                if op.chan is not None:
                    ins.then_inc(op.sem, 16)
                elif op.sig:
                    ins.then_inc(op.sem, 1)
            if ename == "sp":
                for c in final_waits:
                    eng.wait_ge(c.sem, c.count * 16)

        with nc.Block() as block:
            @block.tensor
            def _(eng):
                run("pe", eng)

            @block.scalar
            def _(eng):
                run("act", eng)

            @block.vector
            def _(eng):
                run("dve", eng)

            @block.gpsimd
            def _(eng):
                run("pool", eng)

            @block.sync
            def _(eng):
                run("sp", eng)
        self.nwaits = nwaits[0]


import math
import numpy as np

S = 4096
D = 2048
DFF = 8192
NT = 8
ALPHA = 8 ** 0.25
LN_EPS = 1e-5
GDN_EPS = 1e-6
SUBLN_EPS = 1e-5
H = 16

IN_SPECS = [
    ("x", [S, D]), ("gdn_w_in", [2, D, 8224]), ("gdn_conv_w", [2, 4, 6144]), ("gdn_a_log", [2, 16]),
    ("gdn_dt_bias", [2, 16]), ("gdn_norm_w", [2, 128]), ("gdn_w_out", [2, D, D]), ("diff_w_q", [2, D, D]),
    ("diff_lambda", [2, 4, 64]), ("diff_subln_w", [2, 128]), ("diff_w_o", [2, D, D]),
    ("shared_w_kv", [D, 4096]), ("mlp_w_up", [4, D, DFF]), ("mlp_w_down", [4, DFF, D]),
    ("ln_g", [4, 2, D]), ("ln_b", [4, 2, D]),
]

SLOT = {}
_n = 0
for l in range(2):
    SLOT[("w_in", l)] = _n; _n += 17
    SLOT[("w_out", l)] = _n; _n += 4
SLOT[("w_kv",)] = _n; _n += 8
for j in range(2):
    SLOT[("w_q", j)] = _n; _n += 4
    SLOT[("w_o", j)] = _n; _n += 4
for l in range(4):
    SLOT[("up", l)] = _n; _n += 16
    SLOT[("down", l)] = _n; _n += 16
NSLOT = _n


class K:
    pass


def build(nc, cfg):
    P = Prog(nc)
    k = K()
    k.P = P
    k.nc = nc
    k.cfg = cfg
    k.inp = {n: nc.dram_tensor(n, shp, F32, kind="ExternalInput").ap() for n, shp in IN_SPECS}
    k.out = nc.dram_tensor("out", [S, D], F32, kind="ExternalOutput").ap()
    k.out_tok = [Buf("out%d" % i) for i in range(NT)]
    ex = cfg.get("expose", ())
    kd = lambda n: "ExternalOutput" if n in ex else "Internal"
    WCH = 48
    k.wbf_parts = [nc.dram_tensor("wbf%d" % i, [min(WCH, NSLOT - i * WCH), 128, 8192], BF16, kind=kd("wbf")).ap()
                   for i in range((NSLOT + WCH - 1) // WCH)]
    k.wslot = lambda s: k.wbf_parts[s // WCH][s % WCH]
    k.wbf_tok = [Buf("wbf%d" % i) for i in range(NSLOT)]
    k.xT = nc.dram_tensor("xT", [128, 16, S], BF16, kind=kd("xT")).ap()
    k.xT_tok = [Buf("xT%d" % i) for i in range(NT)]
    k.dbg = {}
    for name, (shp, dt) in cfg.get("dbg", {}).items():
        k.dbg[name] = nc.dram_tensor("dbg_" + name, shp, dt, kind="ExternalOutput").ap()

    k.ident_bf = nc.alloc_sbuf_tensor("ident_bf", [128, 128], BF16)
    k.ident_f = nc.alloc_sbuf_tensor("ident_f", [128, 128], F32)
    k.const_tok = Buf("const")
    P.init_mem(cfg.get("arena", 196 * 1024))
    CT = [k.const_tok]
    P.pool(lambda e: e.memset(k.ident_f[:, :], 1.0), w=CT)
    P.pool(lambda e: e.affine_select(out=k.ident_f[:, :], in_=k.ident_f[:, :], pattern=[[-1, 128]],
                                     compare_op=ALU.is_equal, fill=0.0, base=0, channel_multiplier=1), w=CT)
    P.pool(lambda e: e.tensor_copy(out=k.ident_bf[:, :], in_=k.ident_f[:, :]), w=CT)
    k.neghalf = nc.alloc_sbuf_tensor("neghalf", [128, 16], F32)
    P.pool(lambda e: e.memset(k.neghalf[:, :], -0.5), w=CT)
    k.kT = [nc.dram_tensor("kT%d" % h, [128, S], BF16, kind=kd("kT")).ap() for h in range(16)]
    k.kT_tok = [Buf("kT%d" % i) for i in range(NT)]
    k.vb = nc.dram_tensor("vb", [S, D], BF16, kind=kd("vb")).ap()
    k.vb_tok = [Buf("vb%d" % i) for i in range(NT)]
    k.kmax2 = nc.alloc_sbuf_tensor("kmax2", [128, 32], F32)
    k.kmax_tok = Buf("kmax")
    k.bones = [nc.alloc_sbuf_tensor("bones%d" % c, [128, 128], BF16) for c in range(2)]
    for c in range(2):
        P.pool(lambda e, c=c: e.memset(k.bones[c][:, :], 0.0), w=CT)
        P.pool(lambda e, c=c: e.memset(k.bones[c][c * 64:(c + 1) * 64, :], 1.0), w=CT)
    k.gq = nc.dram_tensor("gq", [S, D], F32, kind=kd("gq")).ap()
    k.gk = nc.dram_tensor("gk", [S, D], F32, kind=kd("gk")).ap()
    k.gv = nc.dram_tensor("gv", [S, D], F32, kind=kd("gv")).ap()
    k.gz = nc.dram_tensor("gz", [S, D], F32, kind=kd("gz")).ap()
    k.gsc = nc.dram_tensor("gsc", [S, 32], F32, kind=kd("gsc")).ap()
    k.gq_tok = [Buf("gq%d" % i) for i in range(NT)]
    k.gz_tok = [Buf("gz%d" % i) for i in range(NT)]
    setup_gdn_consts(k)
    k.obuf = nc.dram_tensor("obuf", [S, D], BF16, kind=kd("obuf")).ap()
    k.obuf_tok = [Buf("obuf%d" % i) for i in range(NT)]

    k.bg = BG(k, cfg["bg"]) if cfg.get("bg") else None
    if cfg.get("wconv", True):
        pass_wconv(k)
    if cfg.get("prep", True):
        pass_prep(k)
    for fn in cfg.get("passes", []):
        fn(k)
    P.emit(final_waits=P.chans)
    return k


def wsrc(k, slot_key, i):
    inp = k.inp
    kind = slot_key[0]
    if kind == "w_in":
        W = inp["gdn_w_in"][slot_key[1]]
        if i < 16:
            return W[:, i * 512:(i + 1) * 512].rearrange("(kc p) n -> p kc n", p=128), 512
        return W[:, 8192:8224].rearrange("(kc p) n -> p kc n", p=128), 32
    if kind in ("w_out", "w_q", "w_o"):
        W = inp[{"w_out": "gdn_w_out", "w_q": "diff_w_q", "w_o": "diff_w_o"}[kind]][slot_key[1]]
        return W[:, i * 512:(i + 1) * 512].rearrange("(kc p) n -> p kc n", p=128), 512
    if kind == "w_kv":
        W = inp["shared_w_kv"]
        return W[:, i * 512:(i + 1) * 512].rearrange("(kc p) n -> p kc n", p=128), 512
    if kind == "up":
        W = inp["mlp_w_up"][slot_key[1]]
        return W[:, i * 512:(i + 1) * 512].rearrange("(kc p) n -> p kc n", p=128), 512
    if kind == "down":
        W = inp["mlp_w_down"][slot_key[1]]
        ob, kg = i // 4, i % 4
        return W[kg * 2048:(kg + 1) * 2048, ob * 512:(ob + 1) * 512].rearrange("(kc p) n -> p kc n", p=128), 512
    raise KeyError(kind)


NSL = {"w_in": 17, "w_out": 4, "w_kv": 8, "w_q": 4, "w_o": 4, "up": 16, "down": 16}


def pass_wconv(k):
    P = k.P
    P.reset()
    st32 = [P.alloc([16, 512], F32, "st32") for _ in range(2)]
    st16 = [P.alloc([16, 512], BF16, "st16") for _ in range(2)]
    chl = [P.chan() for _ in range(2)]
    chs = [P.chan() for _ in range(2)]
    jobs = []
    only = k.cfg.get("wconv_only")
    for key, base in SLOT.items():
        if only is not None and key not in only:
            continue
        for i in range(NSL[key[0]]):
            jobs.append((key, i, base + i))

    def load(n):
        key, i, s = jobs[n]
        src, nc_ = wsrc(k, key, i)
        b = n % 2
        P.dma(st32[b][:, :, 0:nc_], src, chl[b])

    load(0)
    for n, (key, i, s) in enumerate(jobs):
        b = n % 2
        if n + 1 < len(jobs):
            load(n + 1)
        ncols = 32 if (key[0] == "w_in" and i == 16) else 512
        eng = ("act", "dve")[n % 2]
        src = st32[b].ap[:, :, 0:ncols]
        dst = st16[b].ap[:, :, 0:ncols]
        if eng == "act":
            P.act(lambda e, s_=src, d_=dst: e.copy(out=d_, in_=s_), r=[st32[b]], w=[st16[b]])
        elif eng == "dve":
            P.dve(lambda e, s_=src, d_=dst: e.tensor_copy(out=d_, in_=s_), r=[st32[b]], w=[st16[b]])
        else:
            P.pool(lambda e, s_=src, d_=dst: e.tensor_copy(out=d_, in_=s_), r=[st32[b]], w=[st16[b]])
        P.dma(k.wslot(s).rearrange("p (a b) -> p a b", a=16)[:, :, 0:ncols], st16[b][:, :, 0:ncols], chs[b],
              w=[k.wbf_tok[s]])


class BG:
    def __init__(self, k, keys):
        self.k = k
        self.jobs = []
        for key in keys:
            for i in range(NSL[key[0]]):
                for hf in range(4):
                    self.jobs.append((key, i, SLOT[key] + i, hf))
        self.n = 0
        self.nl = 0
        self.ns = 0
        self.bufs = None

    def attach(self):
        P = self.k.P
        assert P.aoff == 0 and P.chan_i == 0
        self.st32 = [P.alloc([4, 512], F32, "bg32") for _ in range(2)]
        self.st16 = [P.alloc([4, 512], BF16, "bg16") for _ in range(2)]
        self.chl = [P.chan() for _ in range(2)]
        self.chs = [P.chan() for _ in range(2)]

    def _ncols(self, j):
        key, i, s, hf = self.jobs[j]
        return 32 if (key[0] == "w_in" and i == 16) else 512

    def _load(self, j):
        key, i, s, hf = self.jobs[j]
        src, ncols = wsrc(self.k, key, i)
        self.k.P.dma(self.st32[j % 2][:, :, 0:ncols], src[:, hf * 4:(hf + 1) * 4, :], self.chl[j % 2])

    def _store(self, j):
        key, i, s, hf = self.jobs[j]
        ncols = self._ncols(j)
        dst = self.k.wslot(s).rearrange("p (a b) -> p a b", a=16)[:, hf * 4:(hf + 1) * 4, 0:ncols]
        self.k.P.dma(dst, self.st16[j % 2][:, :, 0:ncols], self.chs[j % 2], w=[self.k.wbf_tok[s]])

    def step(self):
        P = self.k.P
        if self.n >= len(self.jobs):
            return
        if self.nl <= self.n:
            self._load(self.nl)
            self.nl += 1
        if self.nl < len(self.jobs) and self.nl <= self.n + 1:
            self._load(self.nl)
            self.nl += 1
        j = self.n
        b = j % 2
        ncols = self._ncols(j)
        src = self.st32[b].ap[:, :, 0:ncols]
        dst = self.st16[b].ap[:, :, 0:ncols]
        if self.ns < j - 1:
            self._store(self.ns)
            self.ns += 1
        if j % 2 == 0:
            P.act(lambda e, s_=src, d_=dst: e.copy(out=d_, in_=s_), r=[self.st32[b]], w=[self.st16[b]])
        else:
            P.dve(lambda e, s_=src, d_=dst: e.tensor_copy(out=d_, in_=s_), r=[self.st32[b]], w=[self.st16[b]])
        self.n += 1
        if self.ns < self.n - 1:
            self._store(self.ns)
            self.ns += 1

    def flush(self):
        while self.n < len(self.jobs):
            self.step()
        while self.ns < len(self.jobs):
            self._store(self.ns)
            self.ns += 1


def to_xT(k, xbf, xT_tile, j):
    P = k.P
    for half in range(2):
        bk = P.bank()
        pv = bk.bc(BF16)
        for c in range(8):
            kc = half * 8 + c
            P.pe(lambda e, o=pv.ap[:, c * 128:(c + 1) * 128], i=xbf.ap[:, kc * 128:(kc + 1) * 128]:
                 e.transpose(out=o, in_=i, identity=k.ident_bf[:, :]), r=[xbf, k.const_tok], w=[bk])
        dst = xT_tile.ap[:, half * 8:(half + 1) * 8, j * 128:(j + 1) * 128]
        src = pv.ap.rearrange("p (a b) -> p a b", a=8)
        if half == 0:
            P.act(lambda e, o=dst, i=src: e.copy(out=o, in_=i), r=[bk], w=[xT_tile])
        else:
            P.dve(lambda e, o=dst, i=src: e.tensor_copy(out=o, in_=i), r=[bk], w=[xT_tile])


def pass_prep(k):
    P = k.P
    P.reset()
    xb = [P.alloc([D], F32, "xb") for _ in range(2)]
    xbf = [P.alloc([D], BF16, "xbf") for _ in range(2)]
    xTt = [P.alloc([16, 512], BF16, "xTt") for _ in range(2)]
    chl = [P.chan() for _ in range(2)]
    chs = [P.chan() for _ in range(2)]
    cht = [P.chan() for _ in range(2)]
    x = k.inp["x"]
    nblk = S // 128

    def load(n):
        P.dma(xb[n % 2], x[n * 128:(n + 1) * 128, :], chl[n % 2])

    load(0)
    for n in range(nblk):
        t, j = n // 4, n % 4
        b = n % 2
        if n + 1 < nblk:
            load(n + 1)
        P.dma(k.out[n * 128:(n + 1) * 128, :], xb[b], chs[b], w=[k.out_tok[t]])
        P.act(lambda e, o=xbf[b].ap, i=xb[b].ap: e.copy(out=o, in_=i), r=[xb[b]], w=[xbf[b]])
        to_xT(k, xbf[b], xTt[t % 2], j)
        if j == 3:
            P.dma(k.xT[:, :, t * 512:(t + 1) * 512], xTt[t % 2], cht[t % 2], w=[k.xT_tok[t]])


class WStream:
    def __init__(self, k, ring, chans, seq):
        self.k, self.ring, self.ch, self.seq = k, ring, chans, list(seq)
        self.il = 0
        self.iu = 0

    def prefetch(self):
        k, P = self.k, self.k.P
        R = len(self.ring)
        while self.il < len(self.seq) and self.il < self.iu + R:
            s = self.seq[self.il]
            b = self.il % R
            P.dma(self.ring[b], k.wslot(s).rearrange("p (a b) -> p a b", a=16), self.ch[b], r=[k.wbf_tok[s]])
            self.il += 1

    def next(self):
        self.prefetch()
        v = self.ring[self.iu % len(self.ring)]
        self.iu += 1
        return v


def layer_norm_block(k, y, gb, sm, xbf):
    P = k.P
    stats, mv, rstd, nmr = sm
    for c in range(4):
        P.dve(lambda e, o=stats.ap[:, c, :], i=y.ap[:, c * 512:(c + 1) * 512]: e.bn_stats(out=o, in_=i), r=[y], w=[stats])
    P.dve(lambda e: e.bn_aggr(out=mv.ap, in_=stats.ap.rearrange("p a b -> p (a b)")), r=[stats], w=[mv])
    P.dve(lambda e: e.tensor_scalar(out=rstd.ap, in0=mv.ap[:, 1:2], scalar1=LN_EPS, scalar2=None, op0=ALU.add), r=[mv], w=[rstd])
    P.pool(lambda e: e.tensor_tensor(out=rstd.ap, in0=rstd.ap, in1=k.neghalf[:, 0:1], op=ALU.pow), r=[rstd, k.const_tok], w=[rstd])
    P.dve(lambda e: e.scalar_tensor_tensor(out=nmr.ap, in0=mv.ap[:, 0:1], scalar=-1.0, in1=rstd.ap, op0=ALU.mult, op1=ALU.mult),
          r=[mv, rstd], w=[nmr])
    P.act(lambda e: e.activation(out=y.ap, in_=y.ap, func=AF.Identity, scale=rstd.ap, bias=nmr.ap), r=[y, rstd, nmr], w=[y])
    P.dve(lambda e: e.tensor_tensor(out=y.ap, in0=y.ap, in1=gb.ap[:, 0, :], op=ALU.mult), r=[y, gb], w=[y])
    P.dve(lambda e: e.tensor_tensor(out=y.ap, in0=y.ap, in1=gb.ap[:, 1, :], op=ALU.add), r=[y, gb], w=[y])
    P.act(lambda e: e.copy(out=xbf.ap, in_=y.ap), r=[y], w=[xbf])


def load_gb(k, gb, ch, l, which):
    P = k.P
    P.dma(gb[:, 0, :], k.inp["ln_g"][l, which:which + 1, :].partition_broadcast(128) if False else
          k.inp["ln_g"][l, which, :].partition_broadcast(128), ch)
    P.dma(gb[:, 1, :], k.inp["ln_b"][l, which, :].partition_broadcast(128), ch)


def pass_post(k, l, wo_key, tiles=None):
    P = k.P
    P.reset()
    tiles = list(range(NT)) if tiles is None else list(tiles)
    xblk = [P.alloc([D], F32, "xblk") for _ in range(4)]
    oT = P.alloc([16, 512], BF16, "oT")
    hTall = P.alloc([64, 512], BF16, "hT")
    hT = [V(hTall.ap[:, i, :], hTall.toks[2 * i:2 * i + 2]) for i in range(64)]
    xbf1 = [V(hTall.ap[:, 4 * j:4 * j + 4, :].rearrange("p a b -> p (a b)"), hTall.toks[8 * j:8 * j + 8]) for j in range(2)]
    ring = [P.alloc([16, 512], BF16, "wr") for _ in range(3)]
    gb = P.alloc([2, D], F32, "gb")
    ob = [P.alloc([D], BF16, "ob") for _ in range(2)]
    rtmp = [P.alloc([512], F32, "rtmp") for _ in range(2)]
    sms = [(P.alloc([4, 6], F32), P.alloc([2], F32), P.alloc([1], F32), P.alloc([1], F32)) for _ in range(2)]
    chw = [P.chan() for _ in range(3)]
    chx = [P.chan() for _ in range(4)]
    chg = P.chan()
    cho = [P.chan() for _ in range(2)]
    chso = [P.chan() for _ in range(4)]
    chst = P.chan()
    seq = []
    for t in tiles:
        seq += [SLOT[wo_key] + i for i in range(4)] * 2
        seq += [SLOT[("up", l)] + i for i in range(16)]
        seq += [SLOT[("down", l)] + i for i in range(16)]
    ws = WStream(k, ring, chw, seq)
    for t in tiles:
        ws.prefetch()
        for j in range(4):
            P.dma(xblk[j], k.out[t * 512 + j * 128:t * 512 + (j + 1) * 128, :], chx[j], r=[k.out_tok[t]])
        load_gb(k, gb, chg, l, 0)
        for j in range(4):
            P.dma(ob[j % 2], k.obuf[t * 512 + j * 128:t * 512 + (j + 1) * 128, :], cho[j % 2], r=[k.obuf_tok[t]])
            to_xT(k, ob[j % 2], oT, j)
        for jh in range(2):
            for obk in range(4):
                w = ws.next()
                for j in (2 * jh, 2 * jh + 1):
                    bk = P.bank()
                    for kc in range(16):
                        P.pe(lambda e, o=bk.ap, a=oT.ap[:, kc, j * 128:(j + 1) * 128], b=w.ap[:, kc, :], kc=kc:
                             e.matmul(o, lhsT=a, rhs=b, start=(kc == 0), stop=(kc == 15)), r=[oT, w], w=[bk])
                    xs = xblk[j].ap[:, obk * 512:(obk + 1) * 512]
                    P.dve(lambda e, xs=xs, p=bk.ap: e.scalar_tensor_tensor(out=xs, in0=xs, scalar=ALPHA, in1=p, op0=ALU.mult, op1=ALU.add),
                          r=[xblk[j], bk], w=[xblk[j]])
            if jh == 0:
                for j in (0, 1):
                    layer_norm_block(k, xblk[j], gb, sms[j], xbf1[j])
        for j in (0, 1):
            to_xT(k, xbf1[j], oT, j)
        for j in (2, 3):
            layer_norm_block(k, xblk[j], gb, sms[j % 2], ob[j % 2])
            to_xT(k, ob[j % 2], oT, j)
        if "x1" in k.dbg and t == tiles[0]:
            for j in range(4):
                P.dma(k.dbg["x1"][j * 128:(j + 1) * 128, :], xblk[j], chso[j])
        load_gb(k, gb, chg, l, 1)
        for s in range(16):
            w = ws.next()
            for c in range(4):
                bk = P.bank()
                for kc in range(16):
                    P.pe(lambda e, o=bk.ap, a=w.ap[:, kc, c * 128:(c + 1) * 128], b=oT.ap[:, kc, :], kc=kc:
                         e.matmul(o, lhsT=a, rhs=b, start=(kc == 0), stop=(kc == 15)), r=[oT, w], w=[bk])
                rt = rtmp[(s * 4 + c) % 2]
                h = hT[s * 4 + c]
                P.act(lambda e, o=rt.ap, i=bk.ap: e.activation(out=o, in_=i, func=AF.Relu), r=[bk], w=[rt])
                P.dve(lambda e, o=h.ap, i=rt.ap, p=bk.ap: e.scalar_tensor_tensor(out=o, in0=p, scalar=0.0, in1=i, op0=ALU.max, op1=ALU.mult),
                      r=[rt, bk], w=[h])
        for obk in range(4):
            bks = [P.bank() for _ in range(4)]
            for kg in range(4):
                w = ws.next()
                for j in range(4):
                    for kc in range(16):
                        hh = hT[kg * 16 + kc]
                        P.pe(lambda e, o=bks[j].ap, a=hh.ap[:, j * 128:(j + 1) * 128], b=w.ap[:, kc, :], st=(kg == 0 and kc == 0), sp=(kg == 3 and kc == 15):
                             e.matmul(o, lhsT=a, rhs=b, start=st, stop=sp), r=[hh, w], w=[bks[j]])
            for j in range(4):
                xs = xblk[j].ap[:, obk * 512:(obk + 1) * 512]
                P.dve(lambda e, xs=xs, p=bks[j].ap: e.scalar_tensor_tensor(out=xs, in0=xs, scalar=ALPHA, in1=p, op0=ALU.mult, op1=ALU.add),
                      r=[xblk[j], bks[j]], w=[xblk[j]])
        if "y2" in k.dbg and t == tiles[0]:
            for j in range(4):
                P.dma(k.dbg["y2"][j * 128:(j + 1) * 128, :], xblk[j], chso[j])
        for j in range(4):
            layer_norm_block(k, xblk[j], gb, sms[j % 2], ob[j % 2])
            P.dma(k.out[t * 512 + j * 128:t * 512 + (j + 1) * 128, :], xblk[j], chso[j], w=[k.out_tok[t]])
            to_xT(k, ob[j % 2], oT, j)
        P.dma(k.xT[:, :, t * 512:(t + 1) * 512], oT, chst, w=[k.xT_tok[t]])


def pass_kv(k):
    P = k.P
    P.reset()
    xTt = [P.alloc([16, 512], BF16, "xTt") for _ in range(2)]
    ring = [P.alloc([16, 512], BF16, "wr") for _ in range(3)]
    kst = [P.alloc([512], BF16, "kst") for _ in range(2)]
    sq = [P.alloc([512], BF16, "sq") for _ in range(2)]
    vst = [P.alloc([D], BF16, "vst") for _ in range(4)]
    km = [P.alloc([1], F32, "km") for _ in range(2)]
    chw = [P.chan() for _ in range(3)]
    chx = [P.chan() for _ in range(2)]
    chk = [P.chan() for _ in range(2)]
    chv = [P.chan() for _ in range(4)]
    seq = []
    for t in range(NT):
        seq += [SLOT[("w_kv",)] + i for i in range(8)]
    ws = WStream(k, ring, chw, seq)
    KM = V(k.kmax2[:, :], [k.kmax_tok])
    P.dve(lambda e: e.memset(k.kmax2[:, :], 0.0), w=[KM])
    P.dma(xTt[0], k.xT[:, :, 0:512], chx[0], r=[k.xT_tok[0]])
    n = 0
    for t in range(NT):
        ws.prefetch()
        xt = xTt[t % 2]
        if t + 1 < NT:
            P.dma(xTt[(t + 1) % 2], k.xT[:, :, (t + 1) * 512:(t + 2) * 512], chx[(t + 1) % 2], r=[k.xT_tok[t + 1]])
        for h in range(16):
            if h % 4 == 0:
                w = ws.next()
            bk = P.bank()
            for kc in range(16):
                P.pe(lambda e, o=bk.ap, a=w.ap[:, kc, (h % 4) * 128:(h % 4 + 1) * 128], b=xt.ap[:, kc, :], kc=kc:
                     e.matmul(o, lhsT=a, rhs=b, start=(kc == 0), stop=(kc == 15)), r=[xt, w], w=[bk])
            ks, sqv = kst[n % 2], sq[n % 2]
            n += 1
            P.act(lambda e, o=ks.ap, i=bk.ap: e.copy(out=o, in_=i), r=[bk], w=[ks])
            P.dma(k.kT[h][:, t * 512:(t + 1) * 512], ks, chk[n % 2], w=[k.kT_tok[t]])
            P.dve(lambda e, o=sqv.ap, i=bk.ap, s=ks.ap: e.tensor_tensor(out=o, in0=i, in1=s, op=ALU.mult), r=[bk, ks], w=[sqv])
            for c in range(2):
                b2 = P.bank()
                P.pe(lambda e, o=b2.ap, a=k.bones[c][:, :], b=sqv.ap: e.matmul(o, lhsT=a, rhs=b, start=True, stop=True),
                     r=[sqv, k.const_tok], w=[b2])
                kmv = km[c]
                P.dve(lambda e, o=kmv.ap, i=b2.ap: e.reduce_max(out=o, in_=i, axis=AX.X), r=[b2], w=[kmv])
                col = k.kmax2[:, h * 2 + c:h * 2 + c + 1]
                P.dve(lambda e, o=col, i=kmv.ap: e.tensor_tensor(out=o, in0=o, in1=i, op=ALU.max), r=[kmv, KM], w=[KM])
        for s in range(4):
            w = ws.next()
            for j in range(4):
                bk = P.bank()
                for kc in range(16):
                    P.pe(lambda e, o=bk.ap, a=xt.ap[:, kc, j * 128:(j + 1) * 128], b=w.ap[:, kc, :], kc=kc:
                         e.matmul(o, lhsT=a, rhs=b, start=(kc == 0), stop=(kc == 15)), r=[xt, w], w=[bk])
                dst = vst[j].ap[:, s * 512:(s + 1) * 512]
                if (s + j) % 2 == 0:
                    P.act(lambda e, o=dst, i=bk.ap: e.copy(out=o, in_=i), r=[bk], w=[vst[j]])
                else:
                    P.dve(lambda e, o=dst, i=bk.ap: e.tensor_copy(out=o, in_=i), r=[bk], w=[vst[j]])
        for j in range(4):
            P.dma(k.vb[t * 512 + j * 128:t * 512 + (j + 1) * 128, :], vst[j], chv[j], w=[k.vb_tok[t]])


def pass_attn(k, l, tiles=None):
    P = k.P
    P.reset()
    jl = l - 2
    lam_init = 0.8 - 0.6 * math.exp(-0.3 * l)
    scale = 64 ** -0.5
    tiles = list(range(NT)) if tiles is None else list(tiles)
    xTt = [P.alloc([16, 512], BF16, "xTt") for _ in range(2)]
    ring = [P.alloc([16, 512], BF16, "wr") for _ in range(2)]
    kbuf = [P.alloc([S], BF16, "kbuf") for _ in range(2)]
    vbuf = [P.alloc([32, 130], BF16, "vbuf") for _ in range(2)]
    qz = [[P.alloc([512], BF16, "qz") for _ in range(2)] for _ in range(2)]
    sq = [P.alloc([512], BF16, "sq") for _ in range(2)]
    pT = [P.alloc([512], BF16, "pT") for _ in range(8)]
    pi = [0]
    LA = 3
    oc = [[P.alloc([130], F32, "oc") for _ in range(4)] for _ in range(2)]
    otile = [P.alloc([D], BF16, "otile") for _ in range(4)]
    lp = P.alloc([4, 64], F32, "lp")
    lt = P.alloc([64], F32, "lt")
    lsum = P.alloc([2], F32, "lsum")
    nlam = P.alloc([1], F32, "nlam")
    subw = P.alloc([128], F32, "subw")
    qm = [P.alloc([1], F32, "qm") for _ in range(2)]
    bias = [P.alloc([1], F32, "bias") for _ in range(4)]
    rr = [P.alloc([2], F32, "rr") for _ in range(2)]
    tmp = [P.alloc([128], F32, "tmp") for _ in range(2)]
    o32 = [P.alloc([128], F32, "o32") for _ in range(2)]
    junk = [P.alloc([128], F32, "junk") for _ in range(2)]
    ss = [P.alloc([1], F32, "ss") for _ in range(2)]
    chw = [P.chan() for _ in range(2)]
    chx = [P.chan() for _ in range(2)]
    chk = [P.chan() for _ in range(2)]
    chv = [P.chan() for _ in range(2)]
    cho = [P.chan() for _ in range(4)]
    chc = P.chan()
    KM = V(k.kmax2[:, :], [k.kmax_tok])
    P.dma(lp, k.inp["diff_lambda"][jl].partition_broadcast(128), chc)
    P.dma(subw, k.inp["diff_subln_w"][jl].partition_broadcast(128), chc)
    for i in range(2):
        P.dve(lambda e, i=i: e.tensor_tensor(out=lt.ap, in0=lp.ap[:, 2 * i, :], in1=lp.ap[:, 2 * i + 1, :], op=ALU.mult), r=[lp], w=[lt])
        P.dve(lambda e, i=i: e.reduce_sum(out=lsum.ap[:, i:i + 1], in_=lt.ap, axis=AX.X), r=[lt], w=[lsum])
    P.act(lambda e: e.activation(out=lsum.ap, in_=lsum.ap, func=AF.Exp), r=[lsum], w=[lsum])
    P.dve(lambda e: e.tensor_tensor(out=nlam.ap, in0=lsum.ap[:, 1:2], in1=lsum.ap[:, 0:1], op=ALU.subtract), r=[lsum], w=[nlam])
    P.dve(lambda e: e.tensor_scalar(out=nlam.ap, in0=nlam.ap, scalar1=-lam_init, scalar2=None, op0=ALU.add), r=[nlam], w=[nlam])
    P.act(lambda e: e.mul(out=subw.ap, in_=subw.ap, mul=1.0 - lam_init), r=[subw], w=[subw])
    for b in range(2):
        P.pool(lambda e, b=b: e.memset(vbuf[b].ap[:, :, 128:130], 1.0), w=[vbuf[b]])
    for b in range(2):
        for c in range(2):
            P.pool(lambda e, b=b, c=c: e.memset(qz[b][c].ap, 0.0), w=[qz[b][c]])
    seq = []
    for t in tiles:
        seq += [SLOT[("w_q", jl)] + i for i in range(4)]
    ws = WStream(k, ring, chw, seq)
    OB = [4, 5, 6, 7]
    nb = [0]

    def rbank():
        b = P.banks[OB[nb[0] % 4]]
        nb[0] += 1
        return b

    n = 0
    for ti, t in enumerate(tiles):
        ws.prefetch()
        xt = xTt[ti % 2]
        P.dma(xt, k.xT[:, :, t * 512:(t + 1) * 512], chx[ti % 2], r=[k.xT_tok[t]])
        nk = (t + 1) * 512
        nkb = nk // 128
        for h in range(16):
            if h % 4 == 0:
                w = ws.next()
            kb_, vb_ = kbuf[n % 2], vbuf[n % 2]
            P.dma(kb_[:, 0:nk], k.kT[h][:, 0:nk], chk[n % 2], r=k.kT_tok[0:t + 1])
            P.dma(vb_[:, 0:nkb, 0:128], k.vb[0:nk, h * 128:(h + 1) * 128].rearrange("(kb p) d -> p kb d", p=128), chv[n % 2],
                  r=k.vb_tok[0:t + 1])
            sqv = sq[n % 2]
            bk = rbank()
            for kc in range(16):
                P.pe(lambda e, o=bk.ap, a=w.ap[:, kc, (h % 4) * 128:(h % 4 + 1) * 128], b=xt.ap[:, kc, :], kc=kc:
                     e.matmul(o, lhsT=a, rhs=b, start=(kc == 0), stop=(kc == 15)), r=[xt, w], w=[bk])
            qc = qz[n % 2]
            P.act(lambda e, o=qc[0].ap[0:64, :], i=bk.ap[0:64, :]: e.copy(out=o, in_=i), r=[bk], w=[qc[0]])
            P.act(lambda e, o=qc[1].ap[64:128, :], i=bk.ap[64:128, :]: e.copy(out=o, in_=i), r=[bk], w=[qc[1]])
            P.act(lambda e, o=sqv.ap, i=bk.ap: e.activation(out=o, in_=i, func=AF.Square), r=[bk], w=[sqv])
            for c in range(2):
                b2 = rbank()
                P.pe(lambda e, o=b2.ap, a=k.bones[c][:, :], b=sqv.ap: e.matmul(o, lhsT=a, rhs=b, start=True, stop=True),
                     r=[sqv, k.const_tok], w=[b2])
                P.dve(lambda e, o=qm[c].ap, i=b2.ap: e.reduce_max(out=o, in_=i, axis=AX.X), r=[b2], w=[qm[c]])
                bc = bias[(n % 2) * 2 + c]
                P.dve(lambda e, o=bc.ap, i=qm[c].ap, kc_=k.kmax2[:, h * 2 + c:h * 2 + c + 1]:
                      e.tensor_scalar(out=o, in0=i, scalar1=kc_, scalar2=-scale / 2, op0=ALU.add, op1=ALU.mult), r=[qm[c], KM], w=[bc])
            obk = [P.banks[j] for j in range(4)]
            pend = []
            evac_done = [False, False]

            def emit_qk(c, kb):
                bc = bias[(n % 2) * 2 + c]
                j0 = max(0, kb - 4 * t)
                q0 = j0 * 128
                N = 512 - q0
                sb = rbank()
                P.pe(lambda e, o=sb.ap[:, 0:N], a=kb_.ap[:, kb * 128:(kb + 1) * 128], b=qc[c].ap[:, q0:512]:
                     e.matmul(o, lhsT=a, rhs=b, start=True, stop=True), r=[kb_, qc[c]], w=[sb])
                p = pT[pi[0] % len(pT)]
                pi[0] += 1
                P.act(lambda e, o=p.ap[:, 0:N], i=sb.ap[:, 0:N], bc=bc: e.activation(out=o, in_=i, func=AF.Exp, scale=scale, bias=bc.ap),
                      r=[sb, bc], w=[p])
                if kb >= 4 * t:
                    P.pool(lambda e, o=p.ap[:, 0:128]: e.affine_select(out=o, in_=o, pattern=[[1, 128]], compare_op=ALU.is_ge, fill=0.0,
                                                                        base=0, channel_multiplier=-1), r=[p], w=[p])
                return (c, kb, p, j0)

            def emit_evac(c):
                if not evac_done[c]:
                    evac_done[c] = True
                    for j in range(4):
                        P.dve(lambda e, o=oc[c][j].ap, i=obk[j].ap[:, 0:130]: e.tensor_copy(out=o, in_=i), r=[obk[j]], w=[oc[c][j]])

            def emit_pv(c, kb, p, j0):
                if c == 1:
                    emit_evac(0)
                for j in range(j0, 4):
                    P.pe(lambda e, o=obk[j].ap[:, 0:130], a=p.ap[:, (j - j0) * 128:(j - j0 + 1) * 128], b=vb_.ap[:, kb, :], st=(kb == 0), sp=(kb == 4 * t + j):
                         e.matmul(o, lhsT=a, rhs=b, start=st, stop=sp), r=[p, vb_], w=[obk[j]])

            for c in range(2):
                for kb in range(nkb):
                    pend.append(emit_qk(c, kb))
                    if len(pend) > LA:
                        emit_pv(*pend.pop(0))
            while pend:
                emit_pv(*pend.pop(0))
            emit_evac(0)
            emit_evac(1)
            for j in range(4):
                r_, t_, o_, jk, s_ = rr[j % 2], tmp[j % 2], o32[j % 2], junk[j % 2], ss[j % 2]
                P.dve(lambda e, o=r_.ap[:, 0:1], i=oc[0][j].ap[:, 128:129]: e.reciprocal(out=o, in_=i), r=[oc[0][j]], w=[r_])
                P.dve(lambda e, o=r_.ap[:, 1:2], i=oc[1][j].ap[:, 128:129]: e.reciprocal(out=o, in_=i), r=[oc[1][j]], w=[r_])
                P.dve(lambda e, o=r_.ap[:, 1:2]: e.tensor_tensor(out=o, in0=o, in1=nlam.ap, op=ALU.mult), r=[r_, nlam], w=[r_])
                P.dve(lambda e, o=t_.ap, i=oc[0][j].ap[:, 0:128], s1=r_.ap[:, 0:1]: e.tensor_scalar(out=o, in0=i, scalar1=s1, scalar2=None, op0=ALU.mult),
                      r=[oc[0][j], r_], w=[t_])
                P.dve(lambda e, o=o_.ap, i=oc[1][j].ap[:, 0:128], s1=r_.ap[:, 1:2], t2=t_.ap:
                      e.scalar_tensor_tensor(out=o, in0=i, scalar=s1, in1=t2, op0=ALU.mult, op1=ALU.add), r=[oc[1][j], r_, t_], w=[o_])
                P.act(lambda e, o=jk.ap, i=o_.ap, a=s_.ap: e.activation(out=o, in_=i, func=AF.Square, accum_out=a), r=[o_], w=[jk, s_])
                P.dve(lambda e, o=s_.ap: e.tensor_scalar(out=o, in0=o, scalar1=1.0 / 128, scalar2=SUBLN_EPS, op0=ALU.mult, op1=ALU.add), r=[s_], w=[s_])
                P.pool(lambda e, o=s_.ap: e.tensor_tensor(out=o, in0=o, in1=k.neghalf[:, 0:1], op=ALU.pow), r=[s_, k.const_tok], w=[s_])
                P.dve(lambda e, o=otile[j].ap[:, h * 128:(h + 1) * 128], i=o_.ap, s1=s_.ap:
                      e.scalar_tensor_tensor(out=o, in0=i, scalar=s1, in1=subw.ap, op0=ALU.mult, op1=ALU.mult), r=[o_, s_, subw], w=[otile[j]])
            n += 1
        for j in range(4):
            P.dma(k.obuf[t * 512 + j * 128:t * 512 + (j + 1) * 128, :], otile[j], cho[j], w=[k.obuf_tok[t]])


def setup_gdn_consts(k):
    nc, P = k.nc, k.P
    CT = [k.const_tok]
    k.tri = nc.alloc_sbuf_tensor("tri", [128, 128], F32)
    k.l127 = nc.alloc_sbuf_tensor("l127", [128, 128], F32)
    k.bigm = nc.alloc_sbuf_tensor("bigm", [128, 128], F32)
    k.sel = nc.alloc_sbuf_tensor("sel", [16, 2048], F32)
    P.pool(lambda e: e.memset(k.tri[:, :], 1.0), w=CT)
    P.pool(lambda e: e.affine_select(out=k.tri[:, :], in_=k.tri[:, :], pattern=[[1, 128]], compare_op=ALU.is_ge, fill=0.0,
                                     base=0, channel_multiplier=-1), w=CT)
    P.pool(lambda e: e.memset(k.l127[:, :], 1.0), w=CT)
    P.pool(lambda e: e.affine_select(out=k.l127[:, :], in_=k.l127[:, :], pattern=[[0, 128]], compare_op=ALU.is_ge, fill=0.0,
                                     base=-127, channel_multiplier=1), w=CT)
    P.pool(lambda e: e.memset(k.bigm[:, :], 1e30), w=CT)
    P.pool(lambda e: e.affine_select(out=k.bigm[:, :], in_=k.bigm[:, :], pattern=[[1, 128]], compare_op=ALU.is_gt, fill=0.0,
                                     base=0, channel_multiplier=-1), w=CT)
    P.pool(lambda e: e.memset(k.sel[:, :], 1.0), w=CT)
    P.pool(lambda e: e.affine_select(out=k.sel[:, :], in_=k.sel[:, :], pattern=[[1, 2048]], compare_op=ALU.is_ge, fill=0.0,
                                     base=0, channel_multiplier=-128), w=CT)
    P.pool(lambda e: e.affine_select(out=k.sel[:, :], in_=k.sel[:, :], pattern=[[-1, 2048]], compare_op=ALU.is_ge, fill=0.0,
                                     base=127, channel_multiplier=128), w=CT)


def pass_gdn_a(k, l, tiles=None):
    P = k.P
    P.reset()
    bg = k.bg if l == 0 else None
    if bg is not None:
        bg.attach()
    tiles = list(range(NT)) if tiles is None else list(tiles)
    xTt = [P.alloc([16, 512], BF16, "xTt") for _ in range(2)]
    ring = [P.alloc([16, 512], BF16, "wr") for _ in range(3)]
    NB = 5
    LAG = 2
    nA = [0]
    U = [P.alloc([516], F32, "U") for _ in range(NB)]
    acc = [P.alloc([512], F32, "acc") for _ in range(NB)]
    ysb = [P.alloc([512], F32, "ysb") for _ in range(NB)]
    tqs = [P.alloc([4, 128], F32, "tqs") for _ in range(NB)]
    zs = [P.alloc([512], F32, "zs") for _ in range(2)]
    junk = P.alloc([128], F32, "junk")
    ss4 = [P.alloc([4], F32, "ss4") for _ in range(NB)]
    halo = P.alloc([48, 4], F32, "halo")
    cwT = P.alloc([4, 128], F32, "cwT")
    cw = P.alloc([4, 48], F32, "cw")
    dtb = P.alloc([16], F32, "dtb")
    nea = P.alloc([16], F32, "nea")
    sc = [P.alloc([32], F32, "sc") for _ in range(2)]
    yv = P.alloc([16], F32, "yv")
    ay = P.alloc([16], F32, "ay")
    lv = P.alloc([16], F32, "lv")
    gs = P.alloc([16], F32, "gs")
    chw = [P.chan() for _ in range(3)]
    chx = [P.chan() for _ in range(2)]
    chq = [P.chan() for _ in range(NB)]
    chz = [P.chan() for _ in range(2)]
    chs = [P.chan() for _ in range(2)]
    chc = [P.chan() for _ in range(3)]
    CT = k.const_tok
    P.dma(cwT[0:48, :, :], k.inp["gdn_conv_w"][l].rearrange("j (c p) -> c j p", p=128), chc[0])
    P.dma(dtb, k.inp["gdn_dt_bias"][l].partition_broadcast(128), chc[1])
    P.dma(nea, k.inp["gdn_a_log"][l].partition_broadcast(128), chc[2])
    P.act(lambda e: e.activation(out=nea.ap, in_=nea.ap, func=AF.Exp), r=[nea], w=[nea])
    P.dve(lambda e: e.tensor_scalar(out=nea.ap, in0=nea.ap, scalar1=-1.0, scalar2=None, op0=ALU.mult), r=[nea], w=[nea])
    for j in range(4):
        bk = P.bank()
        P.pe(lambda e, o=bk.ap[:, 0:48], i=cwT.ap[0:48, j, :]: e.transpose(out=o, in_=i, identity=k.ident_f[0:48, 0:48]), r=[cwT, CT], w=[bk])
        P.act(lambda e, o=cw.ap[:, j, :], i=bk.ap[:, 0:48]: e.copy(out=o, in_=i), r=[bk], w=[cw])
    P.dve(lambda e: e.memset(halo.ap, 0.0), w=[halo])
    seq = []
    for t in tiles:
        seq += [SLOT[("w_in", l)] + i for i in range(17)]
    ws = WStream(k, ring, chw, seq)
    P.dma(xTt[0], k.xT[:, :, tiles[0] * 512:(tiles[0] + 1) * 512], chx[0], r=[k.xT_tok[tiles[0]]])
    n = 0
    for ti, t in enumerate(tiles):
        ws.prefetch()
        xt = xTt[ti % 2]
        if ti + 1 < len(tiles):
            t2 = tiles[ti + 1]
            P.dma(xTt[(ti + 1) % 2], k.xT[:, :, t2 * 512:(t2 + 1) * 512], chx[(ti + 1) % 2], r=[k.xT_tok[t2]])
        pendB = []

        def partA(w, c, cc):
            b = nA[0] % NB
            nA[0] += 1
            if bg is not None:
                bg.step()
            bk = P.bank()
            for kc in range(16):
                P.pe(lambda e, o=bk.ap, a=w.ap[:, kc, c * 128:(c + 1) * 128], b_=xt.ap[:, kc, :], kc=kc:
                     e.matmul(o, lhsT=a, rhs=b_, start=(kc == 0), stop=(kc == 15)), r=[xt, w], w=[bk])
            u, ac, y = U[b], acc[b], ysb[b]
            P.act(lambda e, o=u.ap[:, 3:515], i=bk.ap: e.copy(out=o, in_=i), r=[bk], w=[u])
            P.pool(lambda e, o=u.ap[:, 0:3], i=halo.ap[:, cc, 0:3]: e.tensor_copy(out=o, in_=i), r=[halo], w=[u])
            P.act(lambda e, o=ac.ap, i=u.ap[:, 0:512], s1=cw.ap[:, 0, cc:cc + 1]: e.activation(out=o, in_=i, func=AF.Copy, scale=s1),
                  r=[u, cw], w=[ac])
            for j in range(1, 4):
                P.dve(lambda e, o=ac.ap, i=u.ap[:, j:j + 512], s1=cw.ap[:, j, cc:cc + 1]:
                      e.scalar_tensor_tensor(out=o, in0=i, scalar=s1, in1=o, op0=ALU.mult, op1=ALU.add), r=[u, cw, ac], w=[ac])
            P.pool(lambda e, o=halo.ap[:, cc, 0:3], i=u.ap[:, 512:515]: e.tensor_copy(out=o, in_=i), r=[u], w=[halo])
            P.act(lambda e, o=y.ap, i=ac.ap: e.activation(out=o, in_=i, func=AF.Silu), r=[ac], w=[y])
            return (cc, b)

        def partB(cc, b):
            kind, h = cc // 16, cc % 16
            y = ysb[b]
            bt = P.bank()
            for j in range(4):
                P.pe(lambda e, o=bt.ap[:, j * 128:(j + 1) * 128], i=y.ap[:, j * 128:(j + 1) * 128]:
                     e.transpose(out=o, in_=i, identity=k.ident_f[:, :]), r=[y, CT], w=[bt])
            tq = tqs[b]
            if kind < 2:
                s4 = ss4[b]
                for j in range(4):
                    P.act(lambda e, i=bt.ap[:, j * 128:(j + 1) * 128], a=s4.ap[:, j:j + 1]: e.activation(out=junk.ap, in_=i, func=AF.Square, accum_out=a),
                          r=[bt], w=[junk, s4])
                mul = 128.0 if kind == 0 else 1.0
                P.dve(lambda e, o=s4.ap, mul=mul: e.tensor_scalar(out=o, in0=o, scalar1=GDN_EPS, scalar2=mul, op0=ALU.add, op1=ALU.mult), r=[s4], w=[s4])
                P.pool(lambda e, o=s4.ap: e.tensor_tensor(out=o, in0=o, in1=k.neghalf[:, 0:4], op=ALU.pow), r=[s4, CT], w=[s4])
                for j in range(4):
                    P.dve(lambda e, o=tq.ap[:, j, :], i=bt.ap[:, j * 128:(j + 1) * 128], s1=s4.ap[:, j:j + 1]:
                          e.tensor_scalar(out=o, in0=i, scalar1=s1, scalar2=None, op0=ALU.mult), r=[bt, s4], w=[tq])
            else:
                P.act(lambda e, o=tq.ap, i=bt.ap.rearrange("p (a b) -> p a b", a=4): e.copy(out=o, in_=i), r=[bt], w=[tq])
            dst = (k.gq, k.gk, k.gv)[kind]
            P.dma(dst[t * 512:(t + 1) * 512, h * 128:(h + 1) * 128].rearrange("(j p) d -> p j d", p=128), tq, chq[b], w=[k.gq_tok[t]])

        for s in range(12):
            w = ws.next()
            for c in range(4):
                pendB.append(partA(w, c, s * 4 + c))
                if len(pendB) > LAG:
                    partB(*pendB.pop(0))
        while pendB:
            partB(*pendB.pop(0))
        for s in range(4):
            w = ws.next()
            for j in range(4):
                bk = P.bank()
                for kc in range(16):
                    P.pe(lambda e, o=bk.ap, a=xt.ap[:, kc, j * 128:(j + 1) * 128], b_=w.ap[:, kc, :], kc=kc:
                         e.matmul(o, lhsT=a, rhs=b_, start=(kc == 0), stop=(kc == 15)), r=[xt, w], w=[bk])
                z_ = zs[(s * 4 + j) % 2]
                P.act(lambda e, o=z_.ap, i=bk.ap: e.activation(out=o, in_=i, func=AF.Silu), r=[bk], w=[z_])
                P.dma(k.gz[t * 512 + j * 128:t * 512 + (j + 1) * 128, s * 512:(s + 1) * 512], z_, chz[(s * 4 + j) % 2], w=[k.gz_tok[t]])
        w = ws.next()
        for j in range(4):
            bk = P.bank()
            for kc in range(16):
                P.pe(lambda e, o=bk.ap[:, 0:32], a=xt.ap[:, kc, j * 128:(j + 1) * 128], b_=w.ap[:, kc, 0:32], kc=kc:
                     e.matmul(o, lhsT=a, rhs=b_, start=(kc == 0), stop=(kc == 15)), r=[xt, w], w=[bk])
            s_ = sc[j % 2]
            P.act(lambda e, o=s_.ap[:, 0:16], i=bk.ap[:, 0:16]: e.activation(out=o, in_=i, func=AF.Sigmoid), r=[bk], w=[s_])
            P.dve(lambda e, i=bk.ap[:, 16:32]: e.tensor_tensor(out=yv.ap, in0=i, in1=dtb.ap, op=ALU.add), r=[bk, dtb], w=[yv])
            P.dve(lambda e: e.scalar_tensor_tensor(out=ay.ap, in0=yv.ap, scalar=-1.0, in1=yv.ap, op0=ALU.mult, op1=ALU.max), r=[yv], w=[ay])
            P.act(lambda e: e.activation(out=lv.ap, in_=ay.ap, func=AF.Exp, scale=-1.0), r=[ay], w=[lv])
            P.act(lambda e: e.activation(out=lv.ap, in_=lv.ap, func=AF.Ln, bias=1.0), r=[lv], w=[lv])
            P.dve(lambda e: e.scalar_tensor_tensor(out=gs.ap, in0=yv.ap, scalar=0.0, in1=lv.ap, op0=ALU.max, op1=ALU.add), r=[yv, lv], w=[gs])
            P.dve(lambda e: e.tensor_tensor(out=gs.ap, in0=gs.ap, in1=nea.ap, op=ALU.mult), r=[gs, nea], w=[gs])
            b2 = P.bank()
            P.pe(lambda e, o=b2.ap[:, 0:16]: e.matmul(o, lhsT=k.tri[:, :], rhs=gs.ap, start=True, stop=True), r=[gs, CT], w=[b2])
            P.act(lambda e, o=s_.ap[:, 16:32], i=b2.ap[:, 0:16]: e.copy(out=o, in_=i), r=[b2], w=[s_])
            P.dma(k.gsc[t * 512 + j * 128:t * 512 + (j + 1) * 128, :], s_, chs[j % 2], w=[k.gz_tok[t]])


def pass_gdn_b(k, l, chunks=None):
    P = k.P
    P.reset()
    bg = k.bg if l == 0 else None
    if bg is not None:
        bg.attach()
    nst = [0]
    NCH = S // 128
    chunks = list(range(NCH)) if chunks is None else list(chunks)
    G = 8
    Qc = [P.alloc([16, 128], F32, "Qc") for _ in range(2)]
    Kc = [P.alloc([16, 128], F32, "Kc") for _ in range(2)]
    Vc = [P.alloc([16, 128], F32, "Vc") for _ in range(2)]
    Zc = [P.alloc([D], F32, "Zc") for _ in range(2)]
    scb = [P.alloc([32], F32, "scb") for _ in range(2)]
    St = [P.alloc([128], F32, "St") for _ in range(16)]
    och = P.alloc([16, 128], F32, "och")
    obf = [P.alloc([D], BF16, "obf") for _ in range(2)]
    sq_junk = P.alloc([D], F32, "sqj")
    nw = P.alloc([128], F32, "nw")
    sm = {nm: P.alloc([16], F32, nm) for nm in ("nbt", "eg", "gl", "edec", "elast", "nbg", "dif", "ssn")}
    gcT = P.alloc([128], F32, "gcT")
    hb = []
    for g in range(G):
        d = {}
        d["KQ"] = P.alloc([256], F32)
        d["MM"] = [P.alloc([256], F32) for _ in range(2)]
        d["tD"] = P.alloc([128], F32)
        d["Df"] = P.alloc([128], F32)
        d["Ds"] = P.alloc([128], F32)
        d["Am"] = P.alloc([128], F32)
        d["AT"] = P.alloc([128], F32)
        d["Vb"] = P.alloc([128], F32)
        d["Kd"] = P.alloc([128], F32)
        d["Y"] = P.alloc([128], F32)
        d["qs"] = P.alloc([128], F32)
        hb.append(d)
    chq = [P.chan() for _ in range(2)]
    chk = [P.chan() for _ in range(2)]
    chv = [P.chan() for _ in range(2)]
    chz = [P.chan() for _ in range(2)]
    chs = [P.chan() for _ in range(2)]
    cho = [P.chan() for _ in range(2)]
    chc = P.chan()
    CT = k.const_tok
    IDF = k.ident_f[:, :]
    P.dma(nw, k.inp["gdn_norm_w"][l].partition_broadcast(128), chc)
    for h in range(16):
        P.pool(lambda e, o=St[h].ap: e.memset(o, 0.0), w=[St[h]])

    def load(ci):
        n = chunks[ci]
        b = ci % 2
        t = n // 4
        rows = slice(n * 128, (n + 1) * 128)
        P.dma(Qc[b], k.gq[rows, :].rearrange("p (h d) -> p h d", h=16), chq[b], r=[k.gq_tok[t]])
        P.dma(Kc[b], k.gk[rows, :].rearrange("p (h d) -> p h d", h=16), chk[b], r=[k.gq_tok[t]])
        P.dma(Vc[b], k.gv[rows, :].rearrange("p (h d) -> p h d", h=16), chv[b], r=[k.gq_tok[t]])
        P.dma(Zc[b], k.gz[rows, :], chz[b], r=[k.gz_tok[t]])
        P.dma(scb[b], k.gsc[rows, :], chs[b], r=[k.gz_tok[t]])

    def do_chunk(ci, n):
        b = ci % 2
        if ci + 1 < len(chunks):
            load(ci + 1)
        Q_, K_, V_, Z_, sc = Qc[b], Kc[b], Vc[b], Zc[b], scb[b]
        bt = sc.ap[:, 0:16]
        gc = sc.ap[:, 16:32]
        nbt, eg, gl, edec, elast, nbg, dif = (sm[x] for x in ("nbt", "eg", "gl", "edec", "elast", "nbg", "dif"))
        P.dve(lambda e: e.tensor_scalar(out=nbt.ap, in0=bt, scalar1=-1.0, scalar2=None, op0=ALU.mult), r=[sc], w=[nbt])
        P.act(lambda e: e.activation(out=eg.ap, in_=gc, func=AF.Exp), r=[sc], w=[eg])
        bk = P.bank()
        P.pe(lambda e, o=bk.ap[:, 0:16]: e.matmul(o, lhsT=k.l127[:, :], rhs=gc, start=True, stop=True), r=[sc, CT], w=[bk])
        P.act(lambda e, i=bk.ap[:, 0:16]: e.copy(out=gl.ap, in_=i), r=[bk], w=[gl])
        P.dve(lambda e: e.tensor_tensor(out=dif.ap, in0=gl.ap, in1=gc, op=ALU.subtract), r=[gl, sc], w=[dif])
        P.act(lambda e: e.activation(out=edec.ap, in_=dif.ap, func=AF.Exp), r=[dif], w=[edec])
        P.act(lambda e: e.activation(out=elast.ap, in_=gl.ap, func=AF.Exp), r=[gl], w=[elast])
        P.dve(lambda e: e.tensor_tensor(out=nbg.ap, in0=nbt.ap, in1=eg.ap, op=ALU.mult), r=[nbt, eg], w=[nbg])
        bk = P.bank()
        P.pe(lambda e, o=bk.ap[0:16, 0:128]: e.transpose(out=o, in_=gc, identity=IDF), r=[sc, CT], w=[bk])
        P.act(lambda e, i=bk.ap[0:16, 0:128]: e.copy(out=gcT.ap[0:16, :], in_=i), r=[bk], w=[gcT])

        for g0 in range(0, 16, G):
            heads = list(range(g0, g0 + G))
            stages = []

            def stage(fn):
                nst[0] += 1
                if bg is not None and nst[0] % 4 == 0:
                    bg.step()
                for h in heads:
                    fn(h, hb[h - g0])

            def s0(h, d):
                P.act(lambda e: e.activation(out=d["Vb"].ap, in_=V_.ap[:, h, :], func=AF.Copy, scale=bt[:, h:h + 1]),
                      r=[V_, sc], w=[d["Vb"]])
                P.act(lambda e: e.activation(out=d["Kd"].ap, in_=K_.ap[:, h, :], func=AF.Copy, scale=edec.ap[:, h:h + 1]),
                      r=[K_, edec], w=[d["Kd"]])
            stage(s0)

            def s1(h, d):
                bk = P.bank()
                P.pe(lambda e, o=bk.ap[:, 0:128]: e.transpose(out=o, in_=K_.ap[:, h, :], identity=IDF), r=[K_, CT], w=[bk])
                P.pe(lambda e, o=bk.ap[:, 128:256]: e.transpose(out=o, in_=Q_.ap[:, h, :], identity=IDF), r=[Q_, CT], w=[bk])
                P.act(lambda e, i=bk.ap[:, 0:256]: e.copy(out=d["KQ"].ap, in_=i), r=[bk], w=[d["KQ"]])
            stage(s1)

            def s2(h, d):
                bk = P.bank()
                P.pe(lambda e, o=bk.ap[:, 0:128]: e.matmul(o, lhsT=k.sel[:, h * 128:(h + 1) * 128], rhs=gcT.ap[0:16, :], start=True, stop=True),
                     r=[gcT, CT], w=[bk])
                P.dve(lambda e, i=bk.ap[:, 0:128]: e.scalar_tensor_tensor(out=d["tD"].ap, in0=i, scalar=gc[:, h:h + 1], in1=k.bigm[:, :],
                                                                         op0=ALU.subtract, op1=ALU.add), r=[bk, sc, CT], w=[d["tD"]])
                P.act(lambda e: e.activation(out=d["Df"].ap, in_=d["tD"].ap, func=AF.Exp, scale=-1.0), r=[d["tD"]], w=[d["Df"]])
                P.pool(lambda e: e.tensor_tensor(out=d["Ds"].ap, in0=d["Df"].ap, in1=IDF, op=ALU.subtract), r=[d["Df"], CT], w=[d["Ds"]])
            stage(s2)

            def s3(h, d):
                KT = d["KQ"].ap[:, 0:128]
                QT = d["KQ"].ap[:, 128:256]
                bk = P.bank()
                P.pe(lambda e, o=bk.ap[:, 0:128]: e.matmul(o, lhsT=KT, rhs=KT, start=True, stop=True), r=[d["KQ"]], w=[bk])
                P.dve(lambda e, i=bk.ap[:, 0:128]: e.scalar_tensor_tensor(out=d["MM"][0].ap[:, 0:128], in0=i, scalar=nbt.ap[:, h:h + 1], in1=d["Ds"].ap,
                                                                         op0=ALU.mult, op1=ALU.mult), r=[bk, nbt, d["Ds"]], w=[d["MM"][0]])
                b2 = P.bank()
                P.pe(lambda e, o=b2.ap[:, 0:128]: e.matmul(o, lhsT=QT, rhs=KT, start=True, stop=True), r=[d["KQ"]], w=[b2])
                P.dve(lambda e, i=b2.ap[:, 0:128]: e.tensor_tensor(out=d["Am"].ap, in0=i, in1=d["Df"].ap, op=ALU.mult), r=[b2, d["Df"]], w=[d["Am"]])
            stage(s3)

            def s4(h, d):
                bk = P.bank()
                P.pe(lambda e, o=bk.ap[:, 0:128]: e.transpose(out=o, in_=d["MM"][0].ap[:, 0:128], identity=IDF), r=[d["MM"][0], CT], w=[bk])
                P.pe(lambda e, o=bk.ap[:, 128:256]: e.transpose(out=o, in_=d["Am"].ap, identity=IDF), r=[d["Am"], CT], w=[bk])
                P.act(lambda e, i=bk.ap[:, 0:128]: e.copy(out=d["MM"][0].ap[:, 128:256], in_=i), r=[bk], w=[d["MM"][0]])
                P.act(lambda e, i=bk.ap[:, 128:256]: e.copy(out=d["AT"].ap, in_=i), r=[bk], w=[d["AT"]])
            stage(s4)

            def s5(h, d):
                KT = d["KQ"].ap[:, 0:128]
                bk = P.bank()
                P.pe(lambda e, o=bk.ap[:, 0:128]: e.matmul(o, lhsT=KT, rhs=St[h].ap, start=True, stop=True), r=[d["KQ"], St[h]], w=[bk])
                P.dve(lambda e, i=bk.ap[:, 0:128]: e.scalar_tensor_tensor(out=d["Y"].ap, in0=i, scalar=nbg.ap[:, h:h + 1], in1=d["Vb"].ap,
                                                                         op0=ALU.mult, op1=ALU.add), r=[bk, nbg, d["Vb"]], w=[d["Y"]])
            stage(s5)

            for lev in range(7):
                def app(h, d, lev=lev):
                    M = d["MM"][lev % 2]
                    bk = P.bank()
                    P.pe(lambda e, o=bk.ap[:, 0:128]: e.matmul(o, lhsT=M.ap[:, 128:256], rhs=d["Y"].ap, start=True, stop=True), r=[M, d["Y"]], w=[bk])
                    P.dve(lambda e, i=bk.ap[:, 0:128]: e.tensor_tensor(out=d["Y"].ap, in0=d["Y"].ap, in1=i, op=ALU.add), r=[bk, d["Y"]], w=[d["Y"]])
                stage(app)
                if lev < 6:
                    def sqr(h, d, lev=lev):
                        M = d["MM"][lev % 2]
                        M2 = d["MM"][(lev + 1) % 2]
                        bk = P.bank()
                        if lev < 5:
                            P.pe(lambda e, o=bk.ap[:, 0:128]: e.matmul(o, lhsT=M.ap[:, 128:256], rhs=M.ap[:, 0:128], start=True, stop=True), r=[M], w=[bk])
                        P.pe(lambda e, o=bk.ap[:, 128:256]: e.matmul(o, lhsT=M.ap[:, 0:128], rhs=M.ap[:, 128:256], start=True, stop=True), r=[M], w=[bk])
                        if lev < 5:
                            P.act(lambda e, i=bk.ap[:, 0:256]: e.copy(out=M2.ap, in_=i), r=[bk], w=[M2])
                        else:
                            P.act(lambda e, i=bk.ap[:, 128:256]: e.copy(out=M2.ap[:, 128:256], in_=i), r=[bk], w=[M2])
                    stage(sqr)

            def s_out(h, d):
                QT = d["KQ"].ap[:, 128:256]
                bk = P.bank()
                P.pe(lambda e, o=bk.ap[:, 0:128]: e.matmul(o, lhsT=QT, rhs=St[h].ap, start=True, stop=True), r=[d["KQ"], St[h]], w=[bk])
                b2 = P.bank()
                P.pe(lambda e, o=b2.ap[:, 0:128]: e.matmul(o, lhsT=d["AT"].ap, rhs=d["Y"].ap, start=True, stop=True), r=[d["AT"], d["Y"]], w=[b2])
                P.act(lambda e, i=bk.ap[:, 0:128]: e.activation(out=d["qs"].ap, in_=i, func=AF.Copy, scale=eg.ap[:, h:h + 1]), r=[bk, eg], w=[d["qs"]])
                P.dve(lambda e, i=b2.ap[:, 0:128]: e.tensor_tensor(out=och.ap[:, h, :], in0=d["qs"].ap, in1=i, op=ALU.add), r=[b2, d["qs"]], w=[och])
            stage(s_out)

            def s_state(h, d):
                bk = P.bank()
                P.pe(lambda e, o=bk.ap[:, 0:128]: e.matmul(o, lhsT=d["Kd"].ap, rhs=d["Y"].ap, start=True, stop=True), r=[d["Kd"], d["Y"]], w=[bk])
                P.dve(lambda e, i=bk.ap[:, 0:128]: e.scalar_tensor_tensor(out=St[h].ap, in0=St[h].ap, scalar=elast.ap[:, h:h + 1], in1=i,
                                                                         op0=ALU.mult, op1=ALU.add), r=[bk, elast, St[h]], w=[St[h]])
            stage(s_state)

        ssn = sm["ssn"]
        P.act(lambda e: e.activation(out=sq_junk.ap, in_=och.ap.rearrange("p a b -> p (a b)"), func=AF.Square), r=[och], w=[sq_junk])
        P.dve(lambda e: e.reduce_sum(out=ssn.ap, in_=sq_junk.ap.rearrange("p (a b) -> p a b", a=16), axis=AX.X), r=[sq_junk], w=[ssn])
        P.dve(lambda e: e.tensor_scalar(out=ssn.ap, in0=ssn.ap, scalar1=1.0 / 128, scalar2=GDN_EPS, op0=ALU.mult, op1=ALU.add), r=[ssn], w=[ssn])
        P.pool(lambda e: e.tensor_tensor(out=ssn.ap, in0=ssn.ap, in1=k.neghalf[:, 0:16], op=ALU.pow), r=[ssn, CT], w=[ssn])
        for h in range(16):
            P.dve(lambda e, h=h: e.scalar_tensor_tensor(out=och.ap[:, h, :], in0=och.ap[:, h, :], scalar=ssn.ap[:, h:h + 1], in1=nw.ap,
                                                         op0=ALU.mult, op1=ALU.mult), r=[och, ssn, nw], w=[och])
        ob_ = obf[ci % 2]
        P.pool(lambda e: e.tensor_tensor(out=ob_.ap, in0=och.ap.rearrange("p a b -> p (a b)"), in1=Z_.ap, op=ALU.mult), r=[och, Z_], w=[ob_])
        P.dma(k.obuf[n * 128:(n + 1) * 128, :], ob_, cho[ci % 2], w=[k.obuf_tok[n // 4]])

    load(0)
    for ci, n in enumerate(chunks):
        do_chunk(ci, n)
    if bg is not None:
        bg.flush()


def full_passes():
    ps = []
    for l in range(2):
        ps.append(lambda k, l=l: pass_gdn_a(k, l))
        ps.append(lambda k, l=l: pass_gdn_b(k, l))
        ps.append(lambda k, l=l: pass_post(k, l, ("w_out", l)))
    ps.append(pass_kv)
    for l in range(2, 4):
        ps.append(lambda k, l=l: pass_attn(k, l))
        ps.append(lambda k, l=l: pass_post(k, l, ("w_o", l - 2)))
    return ps


def full_cfg():
    bgkeys = [("w_out", 0), ("up", 0), ("down", 0), ("w_in", 1), ("w_out", 1), ("up", 1), ("down", 1), ("w_kv",),
              ("w_q", 0), ("w_o", 0), ("up", 2), ("down", 2), ("w_q", 1), ("w_o", 1), ("up", 3), ("down", 3)]
    return dict(passes=full_passes(), wconv_only=[("w_in", 0)], bg=bgkeys)


def kernel(**inputs):
    nc = bass.Bass("TRN2", target_bir_lowering=False)
    build(nc, full_cfg())
    shared = {n: np.ascontiguousarray(np.asarray(inputs[n], dtype=np.float32)) for n, _ in IN_SPECS if n != "x"}
    x = np.asarray(inputs["x"], dtype=np.float32)
    in_maps = []
    for c in range(8):
        m = dict(shared)
        m["x"] = np.ascontiguousarray(x[c])
        in_maps.append(m)
    res = run_bass_kernel_spmd(nc, in_maps, core_ids=list(range(8)))
    return np.stack([np.asarray(r["out"], dtype=np.float32) for r in res.results], axis=0)
```

```python
import numpy as np
import concourse.bass as bass
import concourse.mybir as mybir
from concourse.bass_utils import run_bass_kernel_spmd

F32 = mybir.dt.float32
BF16 = mybir.dt.bfloat16
U8 = mybir.dt.uint8
AF = mybir.ActivationFunctionType
ALU = mybir.AluOpType
AX = mybir.AxisListType


class Buf:
    __slots__ = ("name", "w", "r")

    def __init__(self, name=""):
        self.name = name
        self.w = None
        self.r = {}


class V:
    __slots__ = ("ap", "toks")

    def __init__(self, ap, toks):
        self.ap = ap
        self.toks = toks

    def __getitem__(self, idx):
        return V(self.ap[idx], self.toks)

    def re(self, pat, **kw):
        return V(self.ap.rearrange(pat, **kw), self.toks)

    def bc(self, dt):
        return V(self.ap.bitcast(dt), self.toks)


class Chan:
    __slots__ = ("sem", "count")

    def __init__(self, sem):
        self.sem = sem
        self.count = 0


class Op:
    __slots__ = ("eng", "fn", "deps", "sem", "val", "sig", "chan", "idx", "dmaw")


ENGS = ("pe", "act", "dve", "pool", "sp")


class Prog:
    def __init__(self, nc):
        self.nc = nc
        self.q = {e: [] for e in ENGS}
        self.esem = {}
        self.nops = 0
        self.chans = []
        self.arena = None
        self.atoks = None
        self.aoff = 0
        self.banks = []
        self.bank_i = 0

    def init_mem(self, arena_bytes, gran=512):
        nc = self.nc
        self.gran = gran
        self.arena_bytes = arena_bytes
        self.arena = nc.alloc_sbuf_tensor("arena", [128, arena_bytes], U8)
        self.atoks = [Buf("a%d" % i) for i in range((arena_bytes + gran - 1) // gran)]
        for i in range(8):
            t = nc.alloc_psum_tensor("bank%d" % i, [128, 512], F32)
            self.banks.append(V(t[:, :], [Buf("bank%d" % i)]))

    def reset(self, off=0):
        self.aoff = off
        self.chan_i = 0

    def alloc(self, free_shape, dt, name=""):
        esz = 2 if dt == BF16 else 4
        n = 1
        for s in free_shape:
            n *= s
        nb = n * esz
        g = self.gran
        off = (self.aoff + g - 1) // g * g
        assert off + nb <= self.arena_bytes, ("SBUF arena overflow", name, off, nb)
        self.aoff = off + nb
        ap = self.arena[:, off:off + nb].bitcast(dt)
        if len(free_shape) == 2:
            ap = ap.rearrange("p (a b) -> p a b", a=free_shape[0])
        elif len(free_shape) == 3:
            ap = ap.rearrange("p (a b c) -> p a b c", a=free_shape[0], b=free_shape[1])
        toks = self.atoks[off // g:(off + nb + g - 1) // g]
        return V(ap, toks)

    def bank(self):
        b = self.banks[self.bank_i % 8]
        self.bank_i += 1
        return b

    def chan(self):
        i = getattr(self, "chan_i", 0)
        if i < len(self.chans):
            c = self.chans[i]
            if c.count > 0:
                op = Op()
                op.eng = "sp"
                op.fn = None
                op.sig = False
                op.chan = None
                op.sem = None
                op.val = 0
                op.idx = self.nops
                self.nops += 1
                op.deps = []
                op.dmaw = [(c.sem, c.count * 16)]
                self.q["sp"].append(op)
        else:
            c = Chan(self.nc.alloc_semaphore("ch%d" % len(self.chans)))
            self.chans.append(c)
        self.chan_i = i + 1
        return c

    def add(self, eng, fn, r=(), w=(), chan=None):
        op = Op()
        op.eng = eng
        op.fn = fn
        op.sig = False
        op.chan = chan
        op.sem = None
        op.val = 0
        op.idx = self.nops
        self.nops += 1
        deps = {}
        wt = []
        for v in w:
            wt.extend(v.toks if isinstance(v, V) else [v])
        rt = []
        for v in r:
            rt.extend(v.toks if isinstance(v, V) else [v])
        for b in rt:
            if b.w is not None:
                deps[b.w.idx] = b.w
        for b in wt:
            if b.w is not None:
                deps[b.w.idx] = b.w
            for o in b.r.values():
                deps[o.idx] = o
        for b in wt:
            b.w = op
            b.r = {}
        wset = set(id(b) for b in wt)
        dma = chan is not None
        for b in rt:
            if id(b) not in wset:
                b.r[(eng, op.idx) if dma else eng] = op
        deps.pop(op.idx, None)
        if eng == "pe":
            op.deps = [d for d in deps.values() if d.eng != "pe"]
        else:
            op.deps = list(deps.values())
        op.dmaw = [(d.chan.sem, d.chan.count * 16) for d in op.deps if d.chan is not None]
        if chan is not None:
            chan.count += 1
            op.sem = chan.sem
            op.val = chan.count * 16
        self.q[eng].append(op)
        return op

    def pe(self, fn, r=(), w=()):
        return self.add("pe", fn, r, w)

    def act(self, fn, r=(), w=()):
        return self.add("act", fn, r, w)

    def dve(self, fn, r=(), w=()):
        return self.add("dve", fn, r, w)

    def pool(self, fn, r=(), w=()):
        return self.add("pool", fn, r, w)

    def dma(self, out, in_, chan, r=(), w=(), **kw):
        oa = out.ap if isinstance(out, V) else out
        ia = in_.ap if isinstance(in_, V) else in_
        rr = list(r) + ([in_] if isinstance(in_, V) else [])
        ww = list(w) + ([out] if isinstance(out, V) else [])
        return self.add("sp", lambda e: e.dma_start(out=oa, in_=ia, **kw), rr, ww, chan)

    def emit(self, final_waits=()):
        nc = self.nc
        for e in ENGS[:4]:
            self.esem[e] = nc.alloc_semaphore("sem_" + e)
        for e in ENGS:
            for op in self.q[e]:
                for d in op.deps:
                    d.sig = True
        for e in ENGS[:4]:
            c = 0
            for op in self.q[e]:
                if op.sig:
                    c += 1
                    op.sem = self.esem[e]
                    op.val = c
        engobj = {"pe": "tensor", "act": "scalar", "dve": "vector", "pool": "gpsimd", "sp": "sync"}
        nwaits = [0]

        def run(ename, eng):
            seen = {}
            for op in self.q[ename]:
                need = {}
                for d in op.deps:
                    if d.chan is not None:
                        continue
                    k = id(d.sem)
                    if d.val > need.get(k, (None, 0))[1]:
                        need[k] = (d.sem, d.val)
                for (s_, v_) in op.dmaw:
                    k = id(s_)
                    if v_ > need.get(k, (None, 0))[1]:
                        need[k] = (s_, v_)
                for k, (s, v) in need.items():
                    if seen.get(k, 0) < v:
                        eng.wait_ge(s, v)
                        seen[k] = v
                        nwaits[0] += 1
                if op.fn is None:
                    continue
                ins = op.fn(eng)
                if op.chan is not None:
                    ins.then_inc(op.sem, 16)
                elif op.sig:
                    ins.then_inc(op.sem, 1)
            if ename == "sp":
                for c in final_waits:
                    eng.wait_ge(c.sem, c.count * 16)

        with nc.Block() as block:
            @block.tensor
            def _(eng):
                run("pe", eng)

            @block.scalar
            def _(eng):
                run("act", eng)

            @block.vector
            def _(eng):
                run("dve", eng)

            @block.gpsimd
            def _(eng):
                run("pool", eng)

            @block.sync
            def _(eng):
                run("sp", eng)
        self.nwaits = nwaits[0]


import math
import numpy as np

S = 4096
D = 2048
DFF = 8192
NT = 8
ALPHA = 8 ** 0.25
LN_EPS = 1e-5
GDN_EPS = 1e-6
SUBLN_EPS = 1e-5
H = 16

IN_SPECS = [
    ("x", [S, D]), ("gdn_w_in", [2, D, 8224]), ("gdn_conv_w", [2, 4, 6144]), ("gdn_a_log", [2, 16]),
    ("gdn_dt_bias", [2, 16]), ("gdn_norm_w", [2, 128]), ("gdn_w_out", [2, D, D]), ("diff_w_q", [2, D, D]),
    ("diff_lambda", [2, 4, 64]), ("diff_subln_w", [2, 128]), ("diff_w_o", [2, D, D]),
    ("shared_w_kv", [D, 4096]), ("mlp_w_up", [4, D, DFF]), ("mlp_w_down", [4, DFF, D]),
    ("ln_g", [4, 2, D]), ("ln_b", [4, 2, D]),
]

SLOT = {}
_n = 0
for l in range(2):
    SLOT[("w_in", l)] = _n; _n += 17
    SLOT[("w_out", l)] = _n; _n += 4
SLOT[("w_kv",)] = _n; _n += 8
for j in range(2):
    SLOT[("w_q", j)] = _n; _n += 4
    SLOT[("w_o", j)] = _n; _n += 4
for l in range(4):
    SLOT[("up", l)] = _n; _n += 16
    SLOT[("down", l)] = _n; _n += 16
NSLOT = _n


class K:
    pass


def build(nc, cfg):
    P = Prog(nc)
    k = K()
    k.P = P
    k.nc = nc
    k.cfg = cfg
    k.inp = {n: nc.dram_tensor(n, shp, F32, kind="ExternalInput").ap() for n, shp in IN_SPECS}
    k.out = nc.dram_tensor("out", [S, D], F32, kind="ExternalOutput").ap()
    k.out_tok = [Buf("out%d" % i) for i in range(NT)]
    ex = cfg.get("expose", ())
    kd = lambda n: "ExternalOutput" if n in ex else "Internal"
    WCH = 48
    k.wbf_parts = [nc.dram_tensor("wbf%d" % i, [min(WCH, NSLOT - i * WCH), 128, 8192], BF16, kind=kd("wbf")).ap()
                   for i in range((NSLOT + WCH - 1) // WCH)]
    k.wslot = lambda s: k.wbf_parts[s // WCH][s % WCH]
    k.wbf_tok = [Buf("wbf%d" % i) for i in range(NSLOT)]
    k.xT = nc.dram_tensor("xT", [128, 16, S], BF16, kind=kd("xT")).ap()
    k.xT_tok = [Buf("xT%d" % i) for i in range(NT)]
    k.dbg = {}
    for name, (shp, dt) in cfg.get("dbg", {}).items():
        k.dbg[name] = nc.dram_tensor("dbg_" + name, shp, dt, kind="ExternalOutput").ap()

    k.ident_bf = nc.alloc_sbuf_tensor("ident_bf", [128, 128], BF16)
    k.ident_f = nc.alloc_sbuf_tensor("ident_f", [128, 128], F32)
    k.const_tok = Buf("const")
    P.init_mem(cfg.get("arena", 196 * 1024))
    CT = [k.const_tok]
    P.pool(lambda e: e.memset(k.ident_f[:, :], 1.0), w=CT)
    P.pool(lambda e: e.affine_select(out=k.ident_f[:, :], in_=k.ident_f[:, :], pattern=[[-1, 128]],
                                     compare_op=ALU.is_equal, fill=0.0, base=0, channel_multiplier=1), w=CT)
    P.pool(lambda e: e.tensor_copy(out=k.ident_bf[:, :], in_=k.ident_f[:, :]), w=CT)
    k.neghalf = nc.alloc_sbuf_tensor("neghalf", [128, 16], F32)
    P.pool(lambda e: e.memset(k.neghalf[:, :], -0.5), w=CT)
    k.kT = [nc.dram_tensor("kT%d" % h, [128, S], BF16, kind=kd("kT")).ap() for h in range(16)]
    k.kT_tok = [Buf("kT%d" % i) for i in range(NT)]
    k.vb = nc.dram_tensor("vb", [S, D], BF16, kind=kd("vb")).ap()
    k.vb_tok = [Buf("vb%d" % i) for i in range(NT)]
    k.kmax2 = nc.alloc_sbuf_tensor("kmax2", [128, 32], F32)
    k.kmax_tok = Buf("kmax")
    k.bones = [nc.alloc_sbuf_tensor("bones%d" % c, [128, 128], BF16) for c in range(2)]
    for c in range(2):
        P.pool(lambda e, c=c: e.memset(k.bones[c][:, :], 0.0), w=CT)
        P.pool(lambda e, c=c: e.memset(k.bones[c][c * 64:(c + 1) * 64, :], 1.0), w=CT)
    k.gq = nc.dram_tensor("gq", [S, D], F32, kind=kd("gq")).ap()
    k.gk = nc.dram_tensor("gk", [S, D], F32, kind=kd("gk")).ap()
    k.gv = nc.dram_tensor("gv", [S, D], F32, kind=kd("gv")).ap()
    k.gz = nc.dram_tensor("gz", [S, D], F32, kind=kd("gz")).ap()
    k.gsc = nc.dram_tensor("gsc", [S, 32], F32, kind=kd("gsc")).ap()
    k.gq_tok = [Buf("gq%d" % i) for i in range(NT)]
    k.gz_tok = [Buf("gz%d" % i) for i in range(NT)]
    setup_gdn_consts(k)
    k.obuf = nc.dram_tensor("obuf", [S, D], BF16, kind=kd("obuf")).ap()
    k.obuf_tok = [Buf("obuf%d" % i) for i in range(NT)]

    k.bg = BG(k, cfg["bg"]) if cfg.get("bg") else None
    if cfg.get("wconv", True):
        pass_wconv(k)
    if cfg.get("prep", True):
        pass_prep(k)
    for fn in cfg.get("passes", []):
        fn(k)
    P.emit(final_waits=P.chans)
    return k


def wsrc(k, slot_key, i):
    inp = k.inp
    kind = slot_key[0]
    if kind == "w_in":
        W = inp["gdn_w_in"][slot_key[1]]
        if i < 16:
            return W[:, i * 512:(i + 1) * 512].rearrange("(kc p) n -> p kc n", p=128), 512
        return W[:, 8192:8224].rearrange("(kc p) n -> p kc n", p=128), 32
    if kind in ("w_out", "w_q", "w_o"):
        W = inp[{"w_out": "gdn_w_out", "w_q": "diff_w_q", "w_o": "diff_w_o"}[kind]][slot_key[1]]
        return W[:, i * 512:(i + 1) * 512].rearrange("(kc p) n -> p kc n", p=128), 512
    if kind == "w_kv":
        W = inp["shared_w_kv"]
        return W[:, i * 512:(i + 1) * 512].rearrange("(kc p) n -> p kc n", p=128), 512
    if kind == "up":
        W = inp["mlp_w_up"][slot_key[1]]
        return W[:, i * 512:(i + 1) * 512].rearrange("(kc p) n -> p kc n", p=128), 512
    if kind == "down":
        W = inp["mlp_w_down"][slot_key[1]]
        ob, kg = i // 4, i % 4
        return W[kg * 2048:(kg + 1) * 2048, ob * 512:(ob + 1) * 512].rearrange("(kc p) n -> p kc n", p=128), 512
    raise KeyError(kind)


NSL = {"w_in": 17, "w_out": 4, "w_kv": 8, "w_q": 4, "w_o": 4, "up": 16, "down": 16}


def pass_wconv(k):
    P = k.P
    P.reset()
    st32 = [P.alloc([16, 512], F32, "st32") for _ in range(2)]
    st16 = [P.alloc([16, 512], BF16, "st16") for _ in range(2)]
    chl = [P.chan() for _ in range(2)]
    chs = [P.chan() for _ in range(2)]
    jobs = []
    only = k.cfg.get("wconv_only")
    for key, base in SLOT.items():
        if only is not None and key not in only:
            continue
        for i in range(NSL[key[0]]):
            jobs.append((key, i, base + i))

    def load(n):
        key, i, s = jobs[n]
        src, nc_ = wsrc(k, key, i)
        b = n % 2
        P.dma(st32[b][:, :, 0:nc_], src, chl[b])

    load(0)
    for n, (key, i, s) in enumerate(jobs):
        b = n % 2
        if n + 1 < len(jobs):
            load(n + 1)
        ncols = 32 if (key[0] == "w_in" and i == 16) else 512
        eng = ("act", "dve")[n % 2]
        src = st32[b].ap[:, :, 0:ncols]
        dst = st16[b].ap[:, :, 0:ncols]
        if eng == "act":
            P.act(lambda e, s_=src, d_=dst: e.copy(out=d_, in_=s_), r=[st32[b]], w=[st16[b]])
        elif eng == "dve":
            P.dve(lambda e, s_=src, d_=dst: e.tensor_copy(out=d_, in_=s_), r=[st32[b]], w=[st16[b]])
        else:
            P.pool(lambda e, s_=src, d_=dst: e.tensor_copy(out=d_, in_=s_), r=[st32[b]], w=[st16[b]])
        P.dma(k.wslot(s).rearrange("p (a b) -> p a b", a=16)[:, :, 0:ncols], st16[b][:, :, 0:ncols], chs[b],
              w=[k.wbf_tok[s]])


class BG:
    def __init__(self, k, keys):
        self.k = k
        self.jobs = []
        for key in keys:
            for i in range(NSL[key[0]]):
                for hf in range(4):
                    self.jobs.append((key, i, SLOT[key] + i, hf))
        self.n = 0
        self.nl = 0
        self.ns = 0
        self.bufs = None

    def attach(self):
        P = self.k.P
        assert P.aoff == 0 and P.chan_i == 0
        self.st32 = [P.alloc([4, 512], F32, "bg32") for _ in range(2)]
        self.st16 = [P.alloc([4, 512], BF16, "bg16") for _ in range(2)]
        self.chl = [P.chan() for _ in range(2)]
        self.chs = [P.chan() for _ in range(2)]

    def _ncols(self, j):
        key, i, s, hf = self.jobs[j]
        return 32 if (key[0] == "w_in" and i == 16) else 512

    def _load(self, j):
        key, i, s, hf = self.jobs[j]
        src, ncols = wsrc(self.k, key, i)
        self.k.P.dma(self.st32[j % 2][:, :, 0:ncols], src[:, hf * 4:(hf + 1) * 4, :], self.chl[j % 2])

    def _store(self, j):
        key, i, s, hf = self.jobs[j]
        ncols = self._ncols(j)
        dst = self.k.wslot(s).rearrange("p (a b) -> p a b", a=16)[:, hf * 4:(hf + 1) * 4, 0:ncols]
        self.k.P.dma(dst, self.st16[j % 2][:, :, 0:ncols], self.chs[j % 2], w=[self.k.wbf_tok[s]])

    def step(self):
        P = self.k.P
        if self.n >= len(self.jobs):
            return
        if self.nl <= self.n:
            self._load(self.nl)
            self.nl += 1
        if self.nl < len(self.jobs) and self.nl <= self.n + 1:
            self._load(self.nl)
            self.nl += 1
        j = self.n
        b = j % 2
        ncols = self._ncols(j)
        src = self.st32[b].ap[:, :, 0:ncols]
        dst = self.st16[b].ap[:, :, 0:ncols]
        if self.ns < j - 1:
            self._store(self.ns)
            self.ns += 1
        if j % 2 == 0:
            P.act(lambda e, s_=src, d_=dst: e.copy(out=d_, in_=s_), r=[self.st32[b]], w=[self.st16[b]])
        else:
            P.dve(lambda e, s_=src, d_=dst: e.tensor_copy(out=d_, in_=s_), r=[self.st32[b]], w=[self.st16[b]])
        self.n += 1
        if self.ns < self.n - 1:
            self._store(self.ns)
            self.ns += 1

    def flush(self):
        while self.n < len(self.jobs):
            self.step()
        while self.ns < len(self.jobs):
            self._store(self.ns)
            self.ns += 1


def to_xT(k, xbf, xT_tile, j):
    P = k.P
    for half in range(2):
        bk = P.bank()
        pv = bk.bc(BF16)
        for c in range(8):
            kc = half * 8 + c
            P.pe(lambda e, o=pv.ap[:, c * 128:(c + 1) * 128], i=xbf.ap[:, kc * 128:(kc + 1) * 128]:
                 e.transpose(out=o, in_=i, identity=k.ident_bf[:, :]), r=[xbf, k.const_tok], w=[bk])
        dst = xT_tile.ap[:, half * 8:(half + 1) * 8, j * 128:(j + 1) * 128]
        src = pv.ap.rearrange("p (a b) -> p a b", a=8)
        if half == 0:
            P.act(lambda e, o=dst, i=src: e.copy(out=o, in_=i), r=[bk], w=[xT_tile])
        else:
            P.dve(lambda e, o=dst, i=src: e.tensor_copy(out=o, in_=i), r=[bk], w=[xT_tile])


def pass_prep(k):
    P = k.P
    P.reset()
    xb = [P.alloc([D], F32, "xb") for _ in range(2)]
    xbf = [P.alloc([D], BF16, "xbf") for _ in range(2)]
    xTt = [P.alloc([16, 512], BF16, "xTt") for _ in range(2)]
    chl = [P.chan() for _ in range(2)]
    chs = [P.chan() for _ in range(2)]
    cht = [P.chan() for _ in range(2)]
    x = k.inp["x"]
    nblk = S // 128

    def load(n):
        P.dma(xb[n % 2], x[n * 128:(n + 1) * 128, :], chl[n % 2])

    load(0)
    for n in range(nblk):
        t, j = n // 4, n % 4
        b = n % 2
        if n + 1 < nblk:
            load(n + 1)
        P.dma(k.out[n * 128:(n + 1) * 128, :], xb[b], chs[b], w=[k.out_tok[t]])
        P.act(lambda e, o=xbf[b].ap, i=xb[b].ap: e.copy(out=o, in_=i), r=[xb[b]], w=[xbf[b]])
        to_xT(k, xbf[b], xTt[t % 2], j)
        if j == 3:
            P.dma(k.xT[:, :, t * 512:(t + 1) * 512], xTt[t % 2], cht[t % 2], w=[k.xT_tok[t]])


class WStream:
    def __init__(self, k, ring, chans, seq):
        self.k, self.ring, self.ch, self.seq = k, ring, chans, list(seq)
        self.il = 0
        self.iu = 0

    def prefetch(self):
        k, P = self.k, self.k.P
        R = len(self.ring)
        while self.il < len(self.seq) and self.il < self.iu + R:
            s = self.seq[self.il]
            b = self.il % R
            P.dma(self.ring[b], k.wslot(s).rearrange("p (a b) -> p a b", a=16), self.ch[b], r=[k.wbf_tok[s]])
            self.il += 1

    def next(self):
        self.prefetch()
        v = self.ring[self.iu % len(self.ring)]
        self.iu += 1
        return v


def layer_norm_block(k, y, gb, sm, xbf):
    P = k.P
    stats, mv, rstd, nmr = sm
    for c in range(4):
        P.dve(lambda e, o=stats.ap[:, c, :], i=y.ap[:, c * 512:(c + 1) * 512]: e.bn_stats(out=o, in_=i), r=[y], w=[stats])
    P.dve(lambda e: e.bn_aggr(out=mv.ap, in_=stats.ap.rearrange("p a b -> p (a b)")), r=[stats], w=[mv])
    P.dve(lambda e: e.tensor_scalar(out=rstd.ap, in0=mv.ap[:, 1:2], scalar1=LN_EPS, scalar2=None, op0=ALU.add), r=[mv], w=[rstd])
    P.pool(lambda e: e.tensor_tensor(out=rstd.ap, in0=rstd.ap, in1=k.neghalf[:, 0:1], op=ALU.pow), r=[rstd, k.const_tok], w=[rstd])
    P.dve(lambda e: e.scalar_tensor_tensor(out=nmr.ap, in0=mv.ap[:, 0:1], scalar=-1.0, in1=rstd.ap, op0=ALU.mult, op1=ALU.mult),
          r=[mv, rstd], w=[nmr])
    P.act(lambda e: e.activation(out=y.ap, in_=y.ap, func=AF.Identity, scale=rstd.ap, bias=nmr.ap), r=[y, rstd, nmr], w=[y])
    P.dve(lambda e: e.tensor_tensor(out=y.ap, in0=y.ap, in1=gb.ap[:, 0, :], op=ALU.mult), r=[y, gb], w=[y])
    P.dve(lambda e: e.tensor_tensor(out=y.ap, in0=y.ap, in1=gb.ap[:, 1, :], op=ALU.add), r=[y, gb], w=[y])
    P.act(lambda e: e.copy(out=xbf.ap, in_=y.ap), r=[y], w=[xbf])


def load_gb(k, gb, ch, l, which):
    P = k.P
    P.dma(gb[:, 0, :], k.inp["ln_g"][l, which:which + 1, :].partition_broadcast(128) if False else
          k.inp["ln_g"][l, which, :].partition_broadcast(128), ch)
    P.dma(gb[:, 1, :], k.inp["ln_b"][l, which, :].partition_broadcast(128), ch)


def pass_post(k, l, wo_key, tiles=None):
    P = k.P
    P.reset()
    tiles = list(range(NT)) if tiles is None else list(tiles)
    xblk = [P.alloc([D], F32, "xblk") for _ in range(4)]
    oT = P.alloc([16, 512], BF16, "oT")
    hTall = P.alloc([64, 512], BF16, "hT")
    hT = [V(hTall.ap[:, i, :], hTall.toks[2 * i:2 * i + 2]) for i in range(64)]
    xbf1 = [V(hTall.ap[:, 4 * j:4 * j + 4, :].rearrange("p a b -> p (a b)"), hTall.toks[8 * j:8 * j + 8]) for j in range(2)]
    ob2 = [V(hTall.ap[:, 8 + 4 * j:12 + 4 * j, :].rearrange("p a b -> p (a b)"), hTall.toks[16 + 8 * j:24 + 8 * j]) for j in range(4)]
    ring = [P.alloc([16, 512], BF16, "wr") for _ in range(3)]
    gb = P.alloc([2, D], F32, "gb")
    ob = [P.alloc([D], BF16, "ob") for _ in range(2)]
    rtmp = [P.alloc([512], F32, "rtmp") for _ in range(2)]
    sms = [(P.alloc([4, 6], F32), P.alloc([2], F32), P.alloc([1], F32), P.alloc([1], F32)) for _ in range(4)]
    chw = [P.chan() for _ in range(3)]
    chx = [P.chan() for _ in range(4)]
    chg = P.chan()
    cho = [P.chan() for _ in range(2)]
    chso = [P.chan() for _ in range(4)]
    chst = P.chan()
    seq = []
    for t in tiles:
        seq += [SLOT[wo_key] + i for i in range(4)] * 2
        seq += [SLOT[("up", l)] + i for i in range(16)]
        seq += [SLOT[("down", l)] + i for i in range(16)]
    ws = WStream(k, ring, chw, seq)
    for t in tiles:
        ws.prefetch()
        for j in range(4):
            P.dma(xblk[j], k.out[t * 512 + j * 128:t * 512 + (j + 1) * 128, :], chx[j], r=[k.out_tok[t]])
        load_gb(k, gb, chg, l, 0)
        for j in range(4):
            P.dma(ob[j % 2], k.obuf[t * 512 + j * 128:t * 512 + (j + 1) * 128, :], cho[j % 2], r=[k.obuf_tok[t]])
            to_xT(k, ob[j % 2], oT, j)
        for jh in range(2):
            for obk in range(4):
                w = ws.next()
                for j in (2 * jh, 2 * jh + 1):
                    bk = P.bank()
                    for kc in range(16):
                        P.pe(lambda e, o=bk.ap, a=oT.ap[:, kc, j * 128:(j + 1) * 128], b=w.ap[:, kc, :], kc=kc:
                             e.matmul(o, lhsT=a, rhs=b, start=(kc == 0), stop=(kc == 15)), r=[oT, w], w=[bk])
                    xs = xblk[j].ap[:, obk * 512:(obk + 1) * 512]
                    P.dve(lambda e, xs=xs, p=bk.ap: e.scalar_tensor_tensor(out=xs, in0=xs, scalar=ALPHA, in1=p, op0=ALU.mult, op1=ALU.add),
                          r=[xblk[j], bk], w=[xblk[j]])
            if jh == 0:
                for j in (0, 1):
                    layer_norm_block(k, xblk[j], gb, sms[j], xbf1[j])
        for j in (0, 1):
            to_xT(k, xbf1[j], oT, j)
        for j in (2, 3):
            layer_norm_block(k, xblk[j], gb, sms[j % 2], ob[j % 2])
            to_xT(k, ob[j % 2], oT, j)
        if "x1" in k.dbg and t == tiles[0]:
            for j in range(4):
                P.dma(k.dbg["x1"][j * 128:(j + 1) * 128, :], xblk[j], chso[j])
        load_gb(k, gb, chg, l, 1)
        for s in range(16):
            w = ws.next()
            for c in range(4):
                bk = P.bank()
                for kc in range(16):
                    P.pe(lambda e, o=bk.ap, a=w.ap[:, kc, c * 128:(c + 1) * 128], b=oT.ap[:, kc, :], kc=kc:
                         e.matmul(o, lhsT=a, rhs=b, start=(kc == 0), stop=(kc == 15)), r=[oT, w], w=[bk])
                rt = rtmp[(s * 4 + c) % 2]
                h = hT[s * 4 + c]
                P.act(lambda e, o=rt.ap, i=bk.ap: e.activation(out=o, in_=i, func=AF.Relu), r=[bk], w=[rt])
                P.dve(lambda e, o=h.ap, i=rt.ap, p=bk.ap: e.scalar_tensor_tensor(out=o, in0=p, scalar=0.0, in1=i, op0=ALU.max, op1=ALU.mult),
                      r=[rt, bk], w=[h])
        for obk in range(4):
            bks = [P.bank() for _ in range(4)]
            for kg in range(4):
                w = ws.next()
                for j in range(4):
                    for kc in range(16):
                        hh = hT[kg * 16 + kc]
                        P.pe(lambda e, o=bks[j].ap, a=hh.ap[:, j * 128:(j + 1) * 128], b=w.ap[:, kc, :], st=(kg == 0 and kc == 0), sp=(kg == 3 and kc == 15):
                             e.matmul(o, lhsT=a, rhs=b, start=st, stop=sp), r=[hh, w], w=[bks[j]])
            for j in range(4):
                xs = xblk[j].ap[:, obk * 512:(obk + 1) * 512]
                P.dve(lambda e, xs=xs, p=bks[j].ap: e.scalar_tensor_tensor(out=xs, in0=xs, scalar=ALPHA, in1=p, op0=ALU.mult, op1=ALU.add),
                      r=[xblk[j], bks[j]], w=[xblk[j]])
        if "y2" in k.dbg and t == tiles[0]:
            for j in range(4):
                P.dma(k.dbg["y2"][j * 128:(j + 1) * 128, :], xblk[j], chso[j])
        for j in range(4):
            layer_norm_block(k, xblk[j], gb, sms[j], ob2[j])
            P.dma(k.out[t * 512 + j * 128:t * 512 + (j + 1) * 128, :], xblk[j], chso[j], w=[k.out_tok[t]])
            to_xT(k, ob2[j], oT, j)
        P.dma(k.xT[:, :, t * 512:(t + 1) * 512], oT, chst, w=[k.xT_tok[t]])


def pass_kv(k):
    P = k.P
    P.reset()
    xTt = [P.alloc([16, 512], BF16, "xTt") for _ in range(2)]
    ring = [P.alloc([16, 512], BF16, "wr") for _ in range(3)]
    kst = [P.alloc([512], BF16, "kst") for _ in range(2)]
    sq = [P.alloc([512], BF16, "sq") for _ in range(2)]
    vst = [P.alloc([D], BF16, "vst") for _ in range(4)]
    km = [P.alloc([1], F32, "km") for _ in range(2)]
    chw = [P.chan() for _ in range(3)]
    chx = [P.chan() for _ in range(2)]
    chk = [P.chan() for _ in range(2)]
    chv = [P.chan() for _ in range(4)]
    seq = []
    for t in range(NT):
        seq += [SLOT[("w_kv",)] + i for i in range(8)]
    ws = WStream(k, ring, chw, seq)
    KM = V(k.kmax2[:, :], [k.kmax_tok])
    P.dve(lambda e: e.memset(k.kmax2[:, :], 0.0), w=[KM])
    P.dma(xTt[0], k.xT[:, :, 0:512], chx[0], r=[k.xT_tok[0]])
    n = 0
    for t in range(NT):
        ws.prefetch()
        xt = xTt[t % 2]
        if t + 1 < NT:
            P.dma(xTt[(t + 1) % 2], k.xT[:, :, (t + 1) * 512:(t + 2) * 512], chx[(t + 1) % 2], r=[k.xT_tok[t + 1]])
        for h in range(16):
            if h % 4 == 0:
                w = ws.next()
            bk = P.bank()
            for kc in range(16):
                P.pe(lambda e, o=bk.ap, a=w.ap[:, kc, (h % 4) * 128:(h % 4 + 1) * 128], b=xt.ap[:, kc, :], kc=kc:
                     e.matmul(o, lhsT=a, rhs=b, start=(kc == 0), stop=(kc == 15)), r=[xt, w], w=[bk])
            ks, sqv = kst[n % 2], sq[n % 2]
            n += 1
            P.act(lambda e, o=ks.ap, i=bk.ap: e.copy(out=o, in_=i), r=[bk], w=[ks])
            P.dma(k.kT[h][:, t * 512:(t + 1) * 512], ks, chk[n % 2], w=[k.kT_tok[t]])
            P.dve(lambda e, o=sqv.ap, i=bk.ap, s=ks.ap: e.tensor_tensor(out=o, in0=i, in1=s, op=ALU.mult), r=[bk, ks], w=[sqv])
            for c in range(2):
                b2 = P.bank()
                P.pe(lambda e, o=b2.ap, a=k.bones[c][:, :], b=sqv.ap: e.matmul(o, lhsT=a, rhs=b, start=True, stop=True),
                     r=[sqv, k.const_tok], w=[b2])
                kmv = km[c]
                P.dve(lambda e, o=kmv.ap, i=b2.ap: e.reduce_max(out=o, in_=i, axis=AX.X), r=[b2], w=[kmv])
                col = k.kmax2[:, h * 2 + c:h * 2 + c + 1]
                P.dve(lambda e, o=col, i=kmv.ap: e.tensor_tensor(out=o, in0=o, in1=i, op=ALU.max), r=[kmv, KM], w=[KM])
        for s in range(4):
            w = ws.next()
            for j in range(4):
                bk = P.bank()
                for kc in range(16):
                    P.pe(lambda e, o=bk.ap, a=xt.ap[:, kc, j * 128:(j + 1) * 128], b=w.ap[:, kc, :], kc=kc:
                         e.matmul(o, lhsT=a, rhs=b, start=(kc == 0), stop=(kc == 15)), r=[xt, w], w=[bk])
                dst = vst[j].ap[:, s * 512:(s + 1) * 512]
                if (s + j) % 2 == 0:
                    P.act(lambda e, o=dst, i=bk.ap: e.copy(out=o, in_=i), r=[bk], w=[vst[j]])
                else:
                    P.dve(lambda e, o=dst, i=bk.ap: e.tensor_copy(out=o, in_=i), r=[bk], w=[vst[j]])
        for j in range(4):
            P.dma(k.vb[t * 512 + j * 128:t * 512 + (j + 1) * 128, :], vst[j], chv[j], w=[k.vb_tok[t]])


def pass_attn(k, l, tiles=None):
    P = k.P
    P.reset()
    jl = l - 2
    lam_init = 0.8 - 0.6 * math.exp(-0.3 * l)
    scale = 64 ** -0.5
    tiles = list(range(NT)) if tiles is None else list(tiles)
    xTt = [P.alloc([16, 512], BF16, "xTt") for _ in range(2)]
    ring = [P.alloc([16, 512], BF16, "wr") for _ in range(2)]
    kbuf = [P.alloc([S], BF16, "kbuf") for _ in range(2)]
    vbuf = [P.alloc([32, 130], BF16, "vbuf") for _ in range(2)]
    qz = [[P.alloc([512], BF16, "qz") for _ in range(2)] for _ in range(2)]
    sq = [P.alloc([512], BF16, "sq") for _ in range(2)]
    pT = [P.alloc([512], BF16, "pT") for _ in range(8)]
    pi = [0]
    LA = 3
    oc = [[P.alloc([130], F32, "oc") for _ in range(4)] for _ in range(2)]
    otile = [P.alloc([D], BF16, "otile") for _ in range(4)]
    lp = P.alloc([4, 64], F32, "lp")
    lt = P.alloc([64], F32, "lt")
    lsum = P.alloc([2], F32, "lsum")
    nlam = P.alloc([1], F32, "nlam")
    subw = P.alloc([128], F32, "subw")
    qm = [P.alloc([1], F32, "qm") for _ in range(2)]
    bias = [P.alloc([1], F32, "bias") for _ in range(4)]
    rr = [P.alloc([2], F32, "rr") for _ in range(2)]
    tmp = [P.alloc([128], F32, "tmp") for _ in range(2)]
    o32 = [P.alloc([128], F32, "o32") for _ in range(2)]
    junk = [P.alloc([128], F32, "junk") for _ in range(2)]
    ss = [P.alloc([1], F32, "ss") for _ in range(2)]
    chw = [P.chan() for _ in range(2)]
    chx = [P.chan() for _ in range(2)]
    chk = [P.chan() for _ in range(2)]
    chv = [P.chan() for _ in range(2)]
    cho = [P.chan() for _ in range(4)]
    chc = P.chan()
    KM = V(k.kmax2[:, :], [k.kmax_tok])
    P.dma(lp, k.inp["diff_lambda"][jl].partition_broadcast(128), chc)
    P.dma(subw, k.inp["diff_subln_w"][jl].partition_broadcast(128), chc)
    for i in range(2):
        P.dve(lambda e, i=i: e.tensor_tensor(out=lt.ap, in0=lp.ap[:, 2 * i, :], in1=lp.ap[:, 2 * i + 1, :], op=ALU.mult), r=[lp], w=[lt])
        P.dve(lambda e, i=i: e.reduce_sum(out=lsum.ap[:, i:i + 1], in_=lt.ap, axis=AX.X), r=[lt], w=[lsum])
    P.act(lambda e: e.activation(out=lsum.ap, in_=lsum.ap, func=AF.Exp), r=[lsum], w=[lsum])
    P.dve(lambda e: e.tensor_tensor(out=nlam.ap, in0=lsum.ap[:, 1:2], in1=lsum.ap[:, 0:1], op=ALU.subtract), r=[lsum], w=[nlam])
    P.dve(lambda e: e.tensor_scalar(out=nlam.ap, in0=nlam.ap, scalar1=-lam_init, scalar2=None, op0=ALU.add), r=[nlam], w=[nlam])
    P.act(lambda e: e.mul(out=subw.ap, in_=subw.ap, mul=1.0 - lam_init), r=[subw], w=[subw])
    for b in range(2):
        P.pool(lambda e, b=b: e.memset(vbuf[b].ap[:, :, 128:130], 1.0), w=[vbuf[b]])
    for b in range(2):
        for c in range(2):
            P.pool(lambda e, b=b, c=c: e.memset(qz[b][c].ap, 0.0), w=[qz[b][c]])
    seq = []
    for t in tiles:
        seq += [SLOT[("w_q", jl)] + i for i in range(4)]
    ws = WStream(k, ring, chw, seq)
    OB = [4, 5, 6, 7]
    nb = [0]

    def rbank():
        b = P.banks[OB[nb[0] % 4]]
        nb[0] += 1
        return b

    n = 0
    for ti, t in enumerate(tiles):
        ws.prefetch()
        xt = xTt[ti % 2]
        P.dma(xt, k.xT[:, :, t * 512:(t + 1) * 512], chx[ti % 2], r=[k.xT_tok[t]])
        nk = (t + 1) * 512
        nkb = nk // 128
        for h in range(16):
            if h % 4 == 0:
                w = ws.next()
            kb_, vb_ = kbuf[n % 2], vbuf[n % 2]
            P.dma(kb_[:, 0:nk], k.kT[h][:, 0:nk], chk[n % 2], r=k.kT_tok[0:t + 1])
            P.dma(vb_[:, 0:nkb, 0:128], k.vb[0:nk, h * 128:(h + 1) * 128].rearrange("(kb p) d -> p kb d", p=128), chv[n % 2],
                  r=k.vb_tok[0:t + 1])
            sqv = sq[n % 2]
            bk = rbank()
            for kc in range(16):
                P.pe(lambda e, o=bk.ap, a=w.ap[:, kc, (h % 4) * 128:(h % 4 + 1) * 128], b=xt.ap[:, kc, :], kc=kc:
                     e.matmul(o, lhsT=a, rhs=b, start=(kc == 0), stop=(kc == 15)), r=[xt, w], w=[bk])
            qc = qz[n % 2]
            P.act(lambda e, o=qc[0].ap[0:64, :], i=bk.ap[0:64, :]: e.copy(out=o, in_=i), r=[bk], w=[qc[0]])
            P.act(lambda e, o=qc[1].ap[64:128, :], i=bk.ap[64:128, :]: e.copy(out=o, in_=i), r=[bk], w=[qc[1]])
            P.act(lambda e, o=sqv.ap, i=bk.ap: e.activation(out=o, in_=i, func=AF.Square), r=[bk], w=[sqv])
            for c in range(2):
                b2 = rbank()
                P.pe(lambda e, o=b2.ap, a=k.bones[c][:, :], b=sqv.ap: e.matmul(o, lhsT=a, rhs=b, start=True, stop=True),
                     r=[sqv, k.const_tok], w=[b2])
                P.dve(lambda e, o=qm[c].ap, i=b2.ap: e.reduce_max(out=o, in_=i, axis=AX.X), r=[b2], w=[qm[c]])
                bc = bias[(n % 2) * 2 + c]
                P.dve(lambda e, o=bc.ap, i=qm[c].ap, kc_=k.kmax2[:, h * 2 + c:h * 2 + c + 1]:
                      e.tensor_scalar(out=o, in0=i, scalar1=kc_, scalar2=-scale / 2, op0=ALU.add, op1=ALU.mult), r=[qm[c], KM], w=[bc])
            obk = [P.banks[j] for j in range(4)]
            pend = []
            evac_done = [False, False]

            def emit_qk(c, kb):
                bc = bias[(n % 2) * 2 + c]
                j0 = max(0, kb - 4 * t)
                q0 = j0 * 128
                N = 512 - q0
                sb = rbank()
                P.pe(lambda e, o=sb.ap[:, 0:N], a=kb_.ap[:, kb * 128:(kb + 1) * 128], b=qc[c].ap[:, q0:512]:
                     e.matmul(o, lhsT=a, rhs=b, start=True, stop=True), r=[kb_, qc[c]], w=[sb])
                p = pT[pi[0] % len(pT)]
                pi[0] += 1
                P.act(lambda e, o=p.ap[:, 0:N], i=sb.ap[:, 0:N], bc=bc: e.activation(out=o, in_=i, func=AF.Exp, scale=scale, bias=bc.ap),
                      r=[sb, bc], w=[p])
                if kb >= 4 * t:
                    P.pool(lambda e, o=p.ap[:, 0:128]: e.affine_select(out=o, in_=o, pattern=[[1, 128]], compare_op=ALU.is_ge, fill=0.0,
                                                                        base=0, channel_multiplier=-1), r=[p], w=[p])
                return (c, kb, p, j0)

            def emit_evac(c):
                if not evac_done[c]:
                    evac_done[c] = True
                    for j in range(4):
                        P.dve(lambda e, o=oc[c][j].ap, i=obk[j].ap[:, 0:130]: e.tensor_copy(out=o, in_=i), r=[obk[j]], w=[oc[c][j]])

            def emit_pv(c, kb, p, j0):
                if c == 1:
                    emit_evac(0)
                for j in range(j0, 4):
                    P.pe(lambda e, o=obk[j].ap[:, 0:130], a=p.ap[:, (j - j0) * 128:(j - j0 + 1) * 128], b=vb_.ap[:, kb, :], st=(kb == 0), sp=(kb == 4 * t + j):
                         e.matmul(o, lhsT=a, rhs=b, start=st, stop=sp), r=[p, vb_], w=[obk[j]])

            for c in range(2):
                for kb in range(nkb):
                    pend.append(emit_qk(c, kb))
                    if len(pend) > LA:
                        emit_pv(*pend.pop(0))
            while pend:
                emit_pv(*pend.pop(0))
            emit_evac(0)
            emit_evac(1)
            for j in range(4):
                r_, t_, o_, jk, s_ = rr[j % 2], tmp[j % 2], o32[j % 2], junk[j % 2], ss[j % 2]
                P.dve(lambda e, o=r_.ap[:, 0:1], i=oc[0][j].ap[:, 128:129]: e.reciprocal(out=o, in_=i), r=[oc[0][j]], w=[r_])
                P.dve(lambda e, o=r_.ap[:, 1:2], i=oc[1][j].ap[:, 128:129]: e.reciprocal(out=o, in_=i), r=[oc[1][j]], w=[r_])
                P.dve(lambda e, o=r_.ap[:, 1:2]: e.tensor_tensor(out=o, in0=o, in1=nlam.ap, op=ALU.mult), r=[r_, nlam], w=[r_])
                P.dve(lambda e, o=t_.ap, i=oc[0][j].ap[:, 0:128], s1=r_.ap[:, 0:1]: e.tensor_scalar(out=o, in0=i, scalar1=s1, scalar2=None, op0=ALU.mult),
                      r=[oc[0][j], r_], w=[t_])
                P.dve(lambda e, o=o_.ap, i=oc[1][j].ap[:, 0:128], s1=r_.ap[:, 1:2], t2=t_.ap:
                      e.scalar_tensor_tensor(out=o, in0=i, scalar=s1, in1=t2, op0=ALU.mult, op1=ALU.add), r=[oc[1][j], r_, t_], w=[o_])
                P.act(lambda e, o=jk.ap, i=o_.ap, a=s_.ap: e.activation(out=o, in_=i, func=AF.Square, accum_out=a), r=[o_], w=[jk, s_])
                P.dve(lambda e, o=s_.ap: e.tensor_scalar(out=o, in0=o, scalar1=1.0 / 128, scalar2=SUBLN_EPS, op0=ALU.mult, op1=ALU.add), r=[s_], w=[s_])
                P.pool(lambda e, o=s_.ap: e.tensor_tensor(out=o, in0=o, in1=k.neghalf[:, 0:1], op=ALU.pow), r=[s_, k.const_tok], w=[s_])
                P.dve(lambda e, o=otile[j].ap[:, h * 128:(h + 1) * 128], i=o_.ap, s1=s_.ap:
                      e.scalar_tensor_tensor(out=o, in0=i, scalar=s1, in1=subw.ap, op0=ALU.mult, op1=ALU.mult), r=[o_, s_, subw], w=[otile[j]])
            n += 1
        for j in range(4):
            P.dma(k.obuf[t * 512 + j * 128:t * 512 + (j + 1) * 128, :], otile[j], cho[j], w=[k.obuf_tok[t]])


def setup_gdn_consts(k):
    nc, P = k.nc, k.P
    CT = [k.const_tok]
    k.tri = nc.alloc_sbuf_tensor("tri", [128, 128], F32)
    k.l127 = nc.alloc_sbuf_tensor("l127", [128, 128], F32)
    k.bigm = nc.alloc_sbuf_tensor("bigm", [128, 128], F32)
    k.sel = nc.alloc_sbuf_tensor("sel", [16, 2048], F32)
    P.pool(lambda e: e.memset(k.tri[:, :], 1.0), w=CT)
    P.pool(lambda e: e.affine_select(out=k.tri[:, :], in_=k.tri[:, :], pattern=[[1, 128]], compare_op=ALU.is_ge, fill=0.0,
                                     base=0, channel_multiplier=-1), w=CT)
    P.pool(lambda e: e.memset(k.l127[:, :], 1.0), w=CT)
    P.pool(lambda e: e.affine_select(out=k.l127[:, :], in_=k.l127[:, :], pattern=[[0, 128]], compare_op=ALU.is_ge, fill=0.0,
                                     base=-127, channel_multiplier=1), w=CT)
    P.pool(lambda e: e.memset(k.bigm[:, :], 1e30), w=CT)
    P.pool(lambda e: e.affine_select(out=k.bigm[:, :], in_=k.bigm[:, :], pattern=[[1, 128]], compare_op=ALU.is_gt, fill=0.0,
                                     base=0, channel_multiplier=-1), w=CT)
    P.pool(lambda e: e.memset(k.sel[:, :], 1.0), w=CT)
    P.pool(lambda e: e.affine_select(out=k.sel[:, :], in_=k.sel[:, :], pattern=[[1, 2048]], compare_op=ALU.is_ge, fill=0.0,
                                     base=0, channel_multiplier=-128), w=CT)
    P.pool(lambda e: e.affine_select(out=k.sel[:, :], in_=k.sel[:, :], pattern=[[-1, 2048]], compare_op=ALU.is_ge, fill=0.0,
                                     base=127, channel_multiplier=128), w=CT)


def pass_gdn_a(k, l, tiles=None):
    P = k.P
    P.reset()
    bg = k.bg if l == 0 else None
    if bg is not None:
        bg.attach()
    tiles = list(range(NT)) if tiles is None else list(tiles)
    xTt = [P.alloc([16, 512], BF16, "xTt") for _ in range(2)]
    ring = [P.alloc([16, 512], BF16, "wr") for _ in range(3)]
    NB = 5
    LAG = 2
    nA = [0]
    U = [P.alloc([516], F32, "U") for _ in range(NB)]
    acc = [P.alloc([512], F32, "acc") for _ in range(NB)]
    ysb = [P.alloc([512], F32, "ysb") for _ in range(NB)]
    tqs = [P.alloc([4, 128], F32, "tqs") for _ in range(NB)]
    zs = [P.alloc([512], F32, "zs") for _ in range(2)]
    junk = P.alloc([128], F32, "junk")
    ss4 = [P.alloc([4], F32, "ss4") for _ in range(NB)]
    halo = P.alloc([48, 4], F32, "halo")
    cwT = P.alloc([4, 128], F32, "cwT")
    cw = P.alloc([4, 48], F32, "cw")
    dtb = P.alloc([16], F32, "dtb")
    nea = P.alloc([16], F32, "nea")
    sc = [P.alloc([32], F32, "sc") for _ in range(2)]
    yv = P.alloc([16], F32, "yv")
    ay = P.alloc([16], F32, "ay")
    lv = P.alloc([16], F32, "lv")
    gs = P.alloc([16], F32, "gs")
    chw = [P.chan() for _ in range(3)]
    chx = [P.chan() for _ in range(2)]
    chq = [P.chan() for _ in range(NB)]
    chz = [P.chan() for _ in range(2)]
    chs = [P.chan() for _ in range(2)]
    chc = [P.chan() for _ in range(3)]
    CT = k.const_tok
    P.dma(cwT[0:48, :, :], k.inp["gdn_conv_w"][l].rearrange("j (c p) -> c j p", p=128), chc[0])
    P.dma(dtb, k.inp["gdn_dt_bias"][l].partition_broadcast(128), chc[1])
    P.dma(nea, k.inp["gdn_a_log"][l].partition_broadcast(128), chc[2])
    P.act(lambda e: e.activation(out=nea.ap, in_=nea.ap, func=AF.Exp), r=[nea], w=[nea])
    P.dve(lambda e: e.tensor_scalar(out=nea.ap, in0=nea.ap, scalar1=-1.0, scalar2=None, op0=ALU.mult), r=[nea], w=[nea])
    for j in range(4):
        bk = P.bank()
        P.pe(lambda e, o=bk.ap[:, 0:48], i=cwT.ap[0:48, j, :]: e.transpose(out=o, in_=i, identity=k.ident_f[0:48, 0:48]), r=[cwT, CT], w=[bk])
        P.act(lambda e, o=cw.ap[:, j, :], i=bk.ap[:, 0:48]: e.copy(out=o, in_=i), r=[bk], w=[cw])
    P.dve(lambda e: e.memset(halo.ap, 0.0), w=[halo])
    seq = []
    for t in tiles:
        seq += [SLOT[("w_in", l)] + i for i in range(17)]
    ws = WStream(k, ring, chw, seq)
    P.dma(xTt[0], k.xT[:, :, tiles[0] * 512:(tiles[0] + 1) * 512], chx[0], r=[k.xT_tok[tiles[0]]])
    n = 0
    for ti, t in enumerate(tiles):
        ws.prefetch()
        xt = xTt[ti % 2]
        if ti + 1 < len(tiles):
            t2 = tiles[ti + 1]
            P.dma(xTt[(ti + 1) % 2], k.xT[:, :, t2 * 512:(t2 + 1) * 512], chx[(ti + 1) % 2], r=[k.xT_tok[t2]])
        pendB = []

        def partA(w, c, cc):
            b = nA[0] % NB
            nA[0] += 1
            if bg is not None:
                bg.step()
            bk = P.bank()
            for kc in range(16):
                P.pe(lambda e, o=bk.ap, a=w.ap[:, kc, c * 128:(c + 1) * 128], b_=xt.ap[:, kc, :], kc=kc:
                     e.matmul(o, lhsT=a, rhs=b_, start=(kc == 0), stop=(kc == 15)), r=[xt, w], w=[bk])
            u, ac, y = U[b], acc[b], ysb[b]
            P.act(lambda e, o=u.ap[:, 3:515], i=bk.ap: e.copy(out=o, in_=i), r=[bk], w=[u])
            P.pool(lambda e, o=u.ap[:, 0:3], i=halo.ap[:, cc, 0:3]: e.tensor_copy(out=o, in_=i), r=[halo], w=[u])
            P.act(lambda e, o=ac.ap, i=u.ap[:, 0:512], s1=cw.ap[:, 0, cc:cc + 1]: e.activation(out=o, in_=i, func=AF.Copy, scale=s1),
                  r=[u, cw], w=[ac])
            for j in range(1, 4):
                P.dve(lambda e, o=ac.ap, i=u.ap[:, j:j + 512], s1=cw.ap[:, j, cc:cc + 1]:
                      e.scalar_tensor_tensor(out=o, in0=i, scalar=s1, in1=o, op0=ALU.mult, op1=ALU.add), r=[u, cw, ac], w=[ac])
            P.pool(lambda e, o=halo.ap[:, cc, 0:3], i=u.ap[:, 512:515]: e.tensor_copy(out=o, in_=i), r=[u], w=[halo])
            P.act(lambda e, o=y.ap, i=ac.ap: e.activation(out=o, in_=i, func=AF.Silu), r=[ac], w=[y])
            return (cc, b)

        def partB(cc, b):
            kind, h = cc // 16, cc % 16
            y = ysb[b]
            bt = P.bank()
            for j in range(4):
                P.pe(lambda e, o=bt.ap[:, j * 128:(j + 1) * 128], i=y.ap[:, j * 128:(j + 1) * 128]:
                     e.transpose(out=o, in_=i, identity=k.ident_f[:, :]), r=[y, CT], w=[bt])
            tq = tqs[b]
            if kind < 2:
                s4 = ss4[b]
                for j in range(4):
                    P.act(lambda e, i=bt.ap[:, j * 128:(j + 1) * 128], a=s4.ap[:, j:j + 1]: e.activation(out=junk.ap, in_=i, func=AF.Square, accum_out=a),
                          r=[bt], w=[junk, s4])
                mul = 128.0 if kind == 0 else 1.0
                P.dve(lambda e, o=s4.ap, mul=mul: e.tensor_scalar(out=o, in0=o, scalar1=GDN_EPS, scalar2=mul, op0=ALU.add, op1=ALU.mult), r=[s4], w=[s4])
                P.pool(lambda e, o=s4.ap: e.tensor_tensor(out=o, in0=o, in1=k.neghalf[:, 0:4], op=ALU.pow), r=[s4, CT], w=[s4])
                for j in range(4):
                    P.dve(lambda e, o=tq.ap[:, j, :], i=bt.ap[:, j * 128:(j + 1) * 128], s1=s4.ap[:, j:j + 1]:
                          e.tensor_scalar(out=o, in0=i, scalar1=s1, scalar2=None, op0=ALU.mult), r=[bt, s4], w=[tq])
            else:
                P.act(lambda e, o=tq.ap, i=bt.ap.rearrange("p (a b) -> p a b", a=4): e.copy(out=o, in_=i), r=[bt], w=[tq])
            dst = (k.gq, k.gk, k.gv)[kind]
            P.dma(dst[t * 512:(t + 1) * 512, h * 128:(h + 1) * 128].rearrange("(j p) d -> p j d", p=128), tq, chq[b], w=[k.gq_tok[t]])

        for s in range(12):
            w = ws.next()
            for c in range(4):
                pendB.append(partA(w, c, s * 4 + c))
                if len(pendB) > LAG:
                    partB(*pendB.pop(0))
        while pendB:
            partB(*pendB.pop(0))
        for s in range(4):
            w = ws.next()
            for j in range(4):
                bk = P.bank()
                for kc in range(16):
                    P.pe(lambda e, o=bk.ap, a=xt.ap[:, kc, j * 128:(j + 1) * 128], b_=w.ap[:, kc, :], kc=kc:
                         e.matmul(o, lhsT=a, rhs=b_, start=(kc == 0), stop=(kc == 15)), r=[xt, w], w=[bk])
                z_ = zs[(s * 4 + j) % 2]
                P.act(lambda e, o=z_.ap, i=bk.ap: e.activation(out=o, in_=i, func=AF.Silu), r=[bk], w=[z_])
                P.dma(k.gz[t * 512 + j * 128:t * 512 + (j + 1) * 128, s * 512:(s + 1) * 512], z_, chz[(s * 4 + j) % 2], w=[k.gz_tok[t]])
        w = ws.next()
        for j in range(4):
            bk = P.bank()
            for kc in range(16):
                P.pe(lambda e, o=bk.ap[:, 0:32], a=xt.ap[:, kc, j * 128:(j + 1) * 128], b_=w.ap[:, kc, 0:32], kc=kc:
                     e.matmul(o, lhsT=a, rhs=b_, start=(kc == 0), stop=(kc == 15)), r=[xt, w], w=[bk])
            s_ = sc[j % 2]
            P.act(lambda e, o=s_.ap[:, 0:16], i=bk.ap[:, 0:16]: e.activation(out=o, in_=i, func=AF.Sigmoid), r=[bk], w=[s_])
            P.dve(lambda e, i=bk.ap[:, 16:32]: e.tensor_tensor(out=yv.ap, in0=i, in1=dtb.ap, op=ALU.add), r=[bk, dtb], w=[yv])
            P.dve(lambda e: e.scalar_tensor_tensor(out=ay.ap, in0=yv.ap, scalar=-1.0, in1=yv.ap, op0=ALU.mult, op1=ALU.max), r=[yv], w=[ay])
            P.act(lambda e: e.activation(out=lv.ap, in_=ay.ap, func=AF.Exp, scale=-1.0), r=[ay], w=[lv])
            P.act(lambda e: e.activation(out=lv.ap, in_=lv.ap, func=AF.Ln, bias=1.0), r=[lv], w=[lv])
            P.dve(lambda e: e.scalar_tensor_tensor(out=gs.ap, in0=yv.ap, scalar=0.0, in1=lv.ap, op0=ALU.max, op1=ALU.add), r=[yv, lv], w=[gs])
            P.dve(lambda e: e.tensor_tensor(out=gs.ap, in0=gs.ap, in1=nea.ap, op=ALU.mult), r=[gs, nea], w=[gs])
            b2 = P.bank()
            P.pe(lambda e, o=b2.ap[:, 0:16]: e.matmul(o, lhsT=k.tri[:, :], rhs=gs.ap, start=True, stop=True), r=[gs, CT], w=[b2])
            P.act(lambda e, o=s_.ap[:, 16:32], i=b2.ap[:, 0:16]: e.copy(out=o, in_=i), r=[b2], w=[s_])
            P.dma(k.gsc[t * 512 + j * 128:t * 512 + (j + 1) * 128, :], s_, chs[j % 2], w=[k.gz_tok[t]])


def pass_gdn_b(k, l, chunks=None):
    P = k.P
    P.reset()
    bg = k.bg if l == 0 else None
    if bg is not None:
        bg.attach()
    nst = [0]
    NCH = S // 128
    chunks = list(range(NCH)) if chunks is None else list(chunks)
    G = 8
    Qc = [P.alloc([16, 128], F32, "Qc") for _ in range(2)]
    Kc = [P.alloc([16, 128], F32, "Kc") for _ in range(2)]
    Vc = [P.alloc([16, 128], F32, "Vc") for _ in range(2)]
    Zc = [P.alloc([D], F32, "Zc") for _ in range(2)]
    scb = [P.alloc([32], F32, "scb") for _ in range(2)]
    St = [P.alloc([128], F32, "St") for _ in range(16)]
    och = P.alloc([16, 128], F32, "och")
    obf = [P.alloc([D], BF16, "obf") for _ in range(2)]
    sq_junk = P.alloc([D], F32, "sqj")
    nw = P.alloc([128], F32, "nw")
    sm = {nm: P.alloc([16], F32, nm) for nm in ("nbt", "eg", "gl", "edec", "elast", "nbg", "dif", "ssn")}
    gcT = P.alloc([128], F32, "gcT")
    hb = []
    for g in range(G):
        d = {}
        d["KQ"] = P.alloc([256], F32)
        d["MM"] = [P.alloc([256], F32) for _ in range(2)]
        d["tD"] = P.alloc([128], F32)
        d["Df"] = P.alloc([128], F32)
        d["Ds"] = P.alloc([128], F32)
        d["Am"] = P.alloc([128], F32)
        d["AT"] = P.alloc([128], F32)
        d["Vb"] = P.alloc([128], F32)
        d["Kd"] = P.alloc([128], F32)
        d["Y"] = P.alloc([128], F32)
        d["qs"] = P.alloc([128], F32)
        hb.append(d)
    chq = [P.chan() for _ in range(2)]
    chk = [P.chan() for _ in range(2)]
    chv = [P.chan() for _ in range(2)]
    chz = [P.chan() for _ in range(2)]
    chs = [P.chan() for _ in range(2)]
    cho = [P.chan() for _ in range(2)]
    chc = P.chan()
    CT = k.const_tok
    IDF = k.ident_f[:, :]
    P.dma(nw, k.inp["gdn_norm_w"][l].partition_broadcast(128), chc)
    for h in range(16):
        P.pool(lambda e, o=St[h].ap: e.memset(o, 0.0), w=[St[h]])

    def load(ci):
        n = chunks[ci]
        b = ci % 2
        t = n // 4
        rows = slice(n * 128, (n + 1) * 128)
        P.dma(Qc[b], k.gq[rows, :].rearrange("p (h d) -> p h d", h=16), chq[b], r=[k.gq_tok[t]])
        P.dma(Kc[b], k.gk[rows, :].rearrange("p (h d) -> p h d", h=16), chk[b], r=[k.gq_tok[t]])
        P.dma(Vc[b], k.gv[rows, :].rearrange("p (h d) -> p h d", h=16), chv[b], r=[k.gq_tok[t]])
        P.dma(Zc[b], k.gz[rows, :], chz[b], r=[k.gz_tok[t]])
        P.dma(scb[b], k.gsc[rows, :], chs[b], r=[k.gz_tok[t]])

    def do_chunk(ci, n):
        b = ci % 2
        if ci + 1 < len(chunks):
            load(ci + 1)
        Q_, K_, V_, Z_, sc = Qc[b], Kc[b], Vc[b], Zc[b], scb[b]
        bt = sc.ap[:, 0:16]
        gc = sc.ap[:, 16:32]
        nbt, eg, gl, edec, elast, nbg, dif = (sm[x] for x in ("nbt", "eg", "gl", "edec", "elast", "nbg", "dif"))
        P.dve(lambda e: e.tensor_scalar(out=nbt.ap, in0=bt, scalar1=-1.0, scalar2=None, op0=ALU.mult), r=[sc], w=[nbt])
        P.act(lambda e: e.activation(out=eg.ap, in_=gc, func=AF.Exp), r=[sc], w=[eg])
        bk = P.bank()
        P.pe(lambda e, o=bk.ap[:, 0:16]: e.matmul(o, lhsT=k.l127[:, :], rhs=gc, start=True, stop=True), r=[sc, CT], w=[bk])
        P.act(lambda e, i=bk.ap[:, 0:16]: e.copy(out=gl.ap, in_=i), r=[bk], w=[gl])
        P.dve(lambda e: e.tensor_tensor(out=dif.ap, in0=gl.ap, in1=gc, op=ALU.subtract), r=[gl, sc], w=[dif])
        P.act(lambda e: e.activation(out=edec.ap, in_=dif.ap, func=AF.Exp), r=[dif], w=[edec])
        P.act(lambda e: e.activation(out=elast.ap, in_=gl.ap, func=AF.Exp), r=[gl], w=[elast])
        P.dve(lambda e: e.tensor_tensor(out=nbg.ap, in0=nbt.ap, in1=eg.ap, op=ALU.mult), r=[nbt, eg], w=[nbg])
        bk = P.bank()
        P.pe(lambda e, o=bk.ap[0:16, 0:128]: e.transpose(out=o, in_=gc, identity=IDF), r=[sc, CT], w=[bk])
        P.act(lambda e, i=bk.ap[0:16, 0:128]: e.copy(out=gcT.ap[0:16, :], in_=i), r=[bk], w=[gcT])

        for g0 in range(0, 16, G):
            heads = list(range(g0, g0 + G))
            stages = []

            def stage(fn):
                nst[0] += 1
                if bg is not None and nst[0] % 4 == 0:
                    bg.step()
                for h in heads:
                    fn(h, hb[h - g0])

            def s0(h, d):
                P.act(lambda e: e.activation(out=d["Vb"].ap, in_=V_.ap[:, h, :], func=AF.Copy, scale=bt[:, h:h + 1]),
                      r=[V_, sc], w=[d["Vb"]])
                P.act(lambda e: e.activation(out=d["Kd"].ap, in_=K_.ap[:, h, :], func=AF.Copy, scale=edec.ap[:, h:h + 1]),
                      r=[K_, edec], w=[d["Kd"]])
            stage(s0)

            def s1(h, d):
                bk = P.bank()
                P.pe(lambda e, o=bk.ap[:, 0:128]: e.transpose(out=o, in_=K_.ap[:, h, :], identity=IDF), r=[K_, CT], w=[bk])
                P.pe(lambda e, o=bk.ap[:, 128:256]: e.transpose(out=o, in_=Q_.ap[:, h, :], identity=IDF), r=[Q_, CT], w=[bk])
                P.act(lambda e, i=bk.ap[:, 0:256]: e.copy(out=d["KQ"].ap, in_=i), r=[bk], w=[d["KQ"]])
            stage(s1)

            def s2(h, d):
                bk = P.bank()
                P.pe(lambda e, o=bk.ap[:, 0:128]: e.matmul(o, lhsT=k.sel[:, h * 128:(h + 1) * 128], rhs=gcT.ap[0:16, :], start=True, stop=True),
                     r=[gcT, CT], w=[bk])
                P.dve(lambda e, i=bk.ap[:, 0:128]: e.scalar_tensor_tensor(out=d["tD"].ap, in0=i, scalar=gc[:, h:h + 1], in1=k.bigm[:, :],
                                                                         op0=ALU.subtract, op1=ALU.add), r=[bk, sc, CT], w=[d["tD"]])
                P.act(lambda e: e.activation(out=d["Df"].ap, in_=d["tD"].ap, func=AF.Exp, scale=-1.0), r=[d["tD"]], w=[d["Df"]])
                P.pool(lambda e: e.tensor_tensor(out=d["Ds"].ap, in0=d["Df"].ap, in1=IDF, op=ALU.subtract), r=[d["Df"], CT], w=[d["Ds"]])
            stage(s2)

            def s3(h, d):
                KT = d["KQ"].ap[:, 0:128]
                QT = d["KQ"].ap[:, 128:256]
                bk = P.bank()
                P.pe(lambda e, o=bk.ap[:, 0:128]: e.matmul(o, lhsT=KT, rhs=KT, start=True, stop=True), r=[d["KQ"]], w=[bk])
                P.dve(lambda e, i=bk.ap[:, 0:128]: e.scalar_tensor_tensor(out=d["MM"][0].ap[:, 0:128], in0=i, scalar=nbt.ap[:, h:h + 1], in1=d["Ds"].ap,
                                                                         op0=ALU.mult, op1=ALU.mult), r=[bk, nbt, d["Ds"]], w=[d["MM"][0]])
                b2 = P.bank()
                P.pe(lambda e, o=b2.ap[:, 0:128]: e.matmul(o, lhsT=QT, rhs=KT, start=True, stop=True), r=[d["KQ"]], w=[b2])
                P.dve(lambda e, i=b2.ap[:, 0:128]: e.tensor_tensor(out=d["Am"].ap, in0=i, in1=d["Df"].ap, op=ALU.mult), r=[b2, d["Df"]], w=[d["Am"]])
            stage(s3)

            def s4(h, d):
                bk = P.bank()
                P.pe(lambda e, o=bk.ap[:, 0:128]: e.transpose(out=o, in_=d["MM"][0].ap[:, 0:128], identity=IDF), r=[d["MM"][0], CT], w=[bk])
                P.pe(lambda e, o=bk.ap[:, 128:256]: e.transpose(out=o, in_=d["Am"].ap, identity=IDF), r=[d["Am"], CT], w=[bk])
                P.act(lambda e, i=bk.ap[:, 0:128]: e.copy(out=d["MM"][0].ap[:, 128:256], in_=i), r=[bk], w=[d["MM"][0]])
                P.act(lambda e, i=bk.ap[:, 128:256]: e.copy(out=d["AT"].ap, in_=i), r=[bk], w=[d["AT"]])
            stage(s4)

            def s5(h, d):
                KT = d["KQ"].ap[:, 0:128]
                bk = P.bank()
                P.pe(lambda e, o=bk.ap[:, 0:128]: e.matmul(o, lhsT=KT, rhs=St[h].ap, start=True, stop=True), r=[d["KQ"], St[h]], w=[bk])
                P.dve(lambda e, i=bk.ap[:, 0:128]: e.scalar_tensor_tensor(out=d["Y"].ap, in0=i, scalar=nbg.ap[:, h:h + 1], in1=d["Vb"].ap,
                                                                         op0=ALU.mult, op1=ALU.add), r=[bk, nbg, d["Vb"]], w=[d["Y"]])
            stage(s5)

            for lev in range(7):
                def app(h, d, lev=lev):
                    M = d["MM"][lev % 2]
                    bk = P.bank()
                    P.pe(lambda e, o=bk.ap[:, 0:128]: e.matmul(o, lhsT=M.ap[:, 128:256], rhs=d["Y"].ap, start=True, stop=True), r=[M, d["Y"]], w=[bk])
                    P.dve(lambda e, i=bk.ap[:, 0:128]: e.tensor_tensor(out=d["Y"].ap, in0=d["Y"].ap, in1=i, op=ALU.add), r=[bk, d["Y"]], w=[d["Y"]])
                stage(app)
                if lev < 6:
                    def sqr(h, d, lev=lev):
                        M = d["MM"][lev % 2]
                        M2 = d["MM"][(lev + 1) % 2]
                        bk = P.bank()
                        if lev < 5:
                            P.pe(lambda e, o=bk.ap[:, 0:128]: e.matmul(o, lhsT=M.ap[:, 128:256], rhs=M.ap[:, 0:128], start=True, stop=True), r=[M], w=[bk])
                        P.pe(lambda e, o=bk.ap[:, 128:256]: e.matmul(o, lhsT=M.ap[:, 0:128], rhs=M.ap[:, 128:256], start=True, stop=True), r=[M], w=[bk])
                        if lev < 5:
                            P.act(lambda e, i=bk.ap[:, 0:256]: e.copy(out=M2.ap, in_=i), r=[bk], w=[M2])
                        else:
                            P.act(lambda e, i=bk.ap[:, 128:256]: e.copy(out=M2.ap[:, 128:256], in_=i), r=[bk], w=[M2])
                    stage(sqr)

            def s_out(h, d):
                QT = d["KQ"].ap[:, 128:256]
                bk = P.bank()
                P.pe(lambda e, o=bk.ap[:, 0:128]: e.matmul(o, lhsT=QT, rhs=St[h].ap, start=True, stop=True), r=[d["KQ"], St[h]], w=[bk])
                b2 = P.bank()
                P.pe(lambda e, o=b2.ap[:, 0:128]: e.matmul(o, lhsT=d["AT"].ap, rhs=d["Y"].ap, start=True, stop=True), r=[d["AT"], d["Y"]], w=[b2])
                P.act(lambda e, i=bk.ap[:, 0:128]: e.activation(out=d["qs"].ap, in_=i, func=AF.Copy, scale=eg.ap[:, h:h + 1]), r=[bk, eg], w=[d["qs"]])
                P.dve(lambda e, i=b2.ap[:, 0:128]: e.tensor_tensor(out=och.ap[:, h, :], in0=d["qs"].ap, in1=i, op=ALU.add), r=[b2, d["qs"]], w=[och])
            stage(s_out)

            def s_state(h, d):
                bk = P.bank()
                P.pe(lambda e, o=bk.ap[:, 0:128]: e.matmul(o, lhsT=d["Kd"].ap, rhs=d["Y"].ap, start=True, stop=True), r=[d["Kd"], d["Y"]], w=[bk])
                P.dve(lambda e, i=bk.ap[:, 0:128]: e.scalar_tensor_tensor(out=St[h].ap, in0=St[h].ap, scalar=elast.ap[:, h:h + 1], in1=i,
                                                                         op0=ALU.mult, op1=ALU.add), r=[bk, elast, St[h]], w=[St[h]])
            stage(s_state)

        ssn = sm["ssn"]
        P.act(lambda e: e.activation(out=sq_junk.ap, in_=och.ap.rearrange("p a b -> p (a b)"), func=AF.Square), r=[och], w=[sq_junk])
        P.dve(lambda e: e.reduce_sum(out=ssn.ap, in_=sq_junk.ap.rearrange("p (a b) -> p a b", a=16), axis=AX.X), r=[sq_junk], w=[ssn])
        P.dve(lambda e: e.tensor_scalar(out=ssn.ap, in0=ssn.ap, scalar1=1.0 / 128, scalar2=GDN_EPS, op0=ALU.mult, op1=ALU.add), r=[ssn], w=[ssn])
        P.pool(lambda e: e.tensor_tensor(out=ssn.ap, in0=ssn.ap, in1=k.neghalf[:, 0:16], op=ALU.pow), r=[ssn, CT], w=[ssn])
        for h in range(16):
            P.dve(lambda e, h=h: e.scalar_tensor_tensor(out=och.ap[:, h, :], in0=och.ap[:, h, :], scalar=ssn.ap[:, h:h + 1], in1=nw.ap,
                                                         op0=ALU.mult, op1=ALU.mult), r=[och, ssn, nw], w=[och])
        ob_ = obf[ci % 2]
        P.pool(lambda e: e.tensor_tensor(out=ob_.ap, in0=och.ap.rearrange("p a b -> p (a b)"), in1=Z_.ap, op=ALU.mult), r=[och, Z_], w=[ob_])
        P.dma(k.obuf[n * 128:(n + 1) * 128, :], ob_, cho[ci % 2], w=[k.obuf_tok[n // 4]])

    load(0)
    for ci, n in enumerate(chunks):
        do_chunk(ci, n)
    if bg is not None:
        bg.flush()


def full_passes():
    ps = []
    for l in range(2):
        ps.append(lambda k, l=l: pass_gdn_a(k, l))
        ps.append(lambda k, l=l: pass_gdn_b(k, l))
        ps.append(lambda k, l=l: pass_post(k, l, ("w_out", l)))
    ps.append(pass_kv)
    for l in range(2, 4):
        ps.append(lambda k, l=l: pass_attn(k, l))
        ps.append(lambda k, l=l: pass_post(k, l, ("w_o", l - 2)))
    return ps


def full_cfg():
    bgkeys = [("w_out", 0), ("up", 0), ("down", 0), ("w_in", 1), ("w_out", 1), ("up", 1), ("down", 1), ("w_kv",),
              ("w_q", 0), ("w_o", 0), ("up", 2), ("down", 2), ("w_q", 1), ("w_o", 1), ("up", 3), ("down", 3)]
    return dict(passes=full_passes(), wconv_only=[("w_in", 0)], bg=bgkeys)


def kernel(**inputs):
    nc = bass.Bass("TRN2", target_bir_lowering=False)
    build(nc, full_cfg())
    shared = {n: np.ascontiguousarray(np.asarray(inputs[n], dtype=np.float32)) for n, _ in IN_SPECS if n != "x"}
    x = np.asarray(inputs["x"], dtype=np.float32)
    in_maps = []
    for c in range(8):
        m = dict(shared)
        m["x"] = np.ascontiguousarray(x[c])
        in_maps.append(m)
    res = run_bass_kernel_spmd(nc, in_maps, core_ids=list(range(8)))
    return np.stack([np.asarray(r["out"], dtype=np.float32) for r in res.results], axis=0)
```
